# Optimizing a Trainium2 kernel written in Bass

```python
import jax, jax.numpy as jnp
from jax import lax
import numpy as np

D_MODEL = 1024
BATCH = 8
SEQ = 2048
DEPTH = 1
DEC_BATCH = 128
DEC_SEQ = 1
PAST_LEN = 16384
PAGE_SIZE = 128

HEAD_SIZE = 64
N_HEADS = D_MODEL // HEAD_SIZE
DECAY_LORA = 64
ICLR_LORA = 64
GATE_LORA = 128
CONV_WIDTH = 3
D_FF = ((8 * D_MODEL // 3 + 127) // 128) * 128
PLE_DIM = 256
SHIFT_COLS = 3 * D_MODEL + DECAY_LORA + ICLR_LORA + GATE_LORA
CONV_COLS = 3 * D_MODEL
GATE_COLS = 2 * D_MODEL
IN_COLS = SHIFT_COLS + CONV_COLS + GATE_COLS
RMS_EPS = 1e-6
GN_EPS = 64e-5
NORM_EPS = 1e-12

kernel_name = "rwkv7_shortconv_gated_hybrid_step"


def rmsnorm(x, g):
    xf = x.astype(jnp.float32)
    y = xf * lax.rsqrt(jnp.mean(xf * xf, axis=-1, keepdims=True) + RMS_EPS) * g.astype(jnp.float32)
    return y.astype(x.dtype)


def swiglu(h, w_gate, w_up, w_down):
    return (jax.nn.silu(h @ w_gate) * (h @ w_up)) @ w_down


def split_cols(z, sizes):
    idx = np.cumsum(sizes)[:-1].tolist()
    return jnp.split(z, idx, axis=-1)


def wkv_step(S, inp):
    r_t, decay_t, k_t, v_t, kk_t, b_t = inp
    sa = jnp.einsum('bhvk,bhk->bhv', S, kk_t)
    S = (S * decay_t[:, :, None, :]
         - sa[..., None] * b_t[:, :, None, :]
         + v_t[..., None] * k_t[:, :, None, :])
    y = jnp.einsum('bhvk,bhk->bhv', S, r_t)
    return S, y


def rwkv7_branch(zs, S0, w_decay_up, decay_bias, w_iclr_up, iclr_bias, w_gate_up,
                 k_k, k_a, r_k, gn_w, gn_b):
    B, T, _ = zs.shape
    r, k, v, wd, ad, gd = split_cols(zs, [D_MODEL, D_MODEL, D_MODEL, DECAY_LORA, ICLR_LORA, GATE_LORA])
    f32 = jnp.float32
    w = -jax.nn.softplus(-(decay_bias + jnp.tanh(wd) @ w_decay_up).astype(f32)) - 0.5
    decay = jnp.exp(-jnp.exp(w))
    a = jax.nn.sigmoid((iclr_bias + ad @ w_iclr_up).astype(f32))
    g = jax.nn.sigmoid(gd) @ w_gate_up
    heads = lambda t: t.astype(f32).reshape(B, T, N_HEADS, HEAD_SIZE)
    kk = heads(k * k_k)
    kk = kk / jnp.maximum(jnp.linalg.norm(kk, axis=-1, keepdims=True), NORM_EPS)
    k_mod = k.astype(f32) * (1.0 + (a - 1.0) * k_a.astype(f32))
    rh, kh, vh, ah, dh = heads(r), heads(k_mod), heads(v), heads(a), heads(decay)
    bh = kk * ah
    tm = lambda t: jnp.moveaxis(t, 1, 0)
    S_new, ys = lax.scan(wkv_step, S0.astype(f32), (tm(rh), tm(dh), tm(kh), tm(vh), tm(kk), tm(bh)))
    y = jnp.moveaxis(ys, 0, 1)
    mean = jnp.mean(y, axis=-1, keepdims=True)
    var = jnp.mean(jnp.square(y - mean), axis=-1, keepdims=True)
    y = (y - mean) * lax.rsqrt(var + GN_EPS)
    y = y.reshape(B, T, D_MODEL) * gn_w.astype(f32) + gn_b.astype(f32)
    bonus = jnp.sum(rh * kh * r_k.astype(f32), axis=-1, keepdims=True) * vh
    y = (y + bonus.reshape(B, T, D_MODEL)).astype(zs.dtype) * g
    return y, S_new


def short_conv_branch(zc, conv0, conv_w):
    gb, gc, u = split_cols(zc, [D_MODEL, D_MODEL, D_MODEL])
    T = zc.shape[1]
    cu = gc * u
    full = jnp.concatenate([conv0.astype(cu.dtype), cu], axis=1)
    y = full[:, 0:T] * conv_w[0]
    for j in range(1, CONV_WIDTH):
        y = y + full[:, j:j + T] * conv_w[j]
    return gb * y, full[:, T:]


def decoder_layer(x, p, S0, shift0, conv0,
                  norm_ffn1, w_ffn1_gate, w_ffn1_up, w_ffn1_down,
                  norm_mix, w_in, mu_shift, w_decay_up, decay_bias, w_iclr_up, iclr_bias,
                  w_gate_up, k_k, k_a, r_k, gn_w, gn_b, conv_w, w_out,
                  norm_ffn2, w_ffn2_gate, w_ffn2_up, w_ffn2_down,
                  norm_ple, w_ple_gate, w_ple_proj):
    x = x + 0.5 * swiglu(rmsnorm(x, norm_ffn1), w_ffn1_gate, w_ffn1_up, w_ffn1_down)
    h = rmsnorm(x, norm_mix)
    z = h @ w_in
    z_sh, z_conv, z_gate = split_cols(z, [SHIFT_COLS, CONV_COLS, GATE_COLS])
    z_prev = jnp.concatenate([shift0[:, None, :].astype(z_sh.dtype), z_sh[:, :-1]], axis=1)
    zs = z_sh + mu_shift * (z_prev - z_sh)
    new_shift = z_sh[:, -1]
    y_a, S_new = rwkv7_branch(zs, S0, w_decay_up, decay_bias, w_iclr_up, iclr_bias, w_gate_up,
                              k_k, k_a, r_k, gn_w, gn_b)
    y_b, conv_new = short_conv_branch(z_conv, conv0, conv_w)
    g_a, g_b = split_cols(jax.nn.sigmoid(z_gate), [D_MODEL, D_MODEL])
    x = x + (g_a * y_a + g_b * y_b) @ w_out
    x = x + 0.5 * swiglu(rmsnorm(x, norm_ffn2), w_ffn2_gate, w_ffn2_up, w_ffn2_down)
    x = x + jax.nn.sigmoid(rmsnorm(x, norm_ple) @ w_ple_gate) * (p @ w_ple_proj)
    return x, S_new, new_shift, conv_new


def setup_inputs(seed: int = 0) -> dict:
    key = jax.random.key(seed)
    ks = iter(jax.random.split(key, 48))
    f32 = jnp.float32
    nrm = lambda shape, scale: jax.random.normal(next(ks), shape, f32) * scale
    gain = lambda shape: 1.0 + 0.05 * jax.random.normal(next(ks), shape, f32)
    L, D = DEPTH, D_MODEL
    return {
        "x_prompt": nrm((BATCH, SEQ, D), 1.0),
        "x_sample": nrm((DEC_BATCH, DEC_SEQ, D), 1.0),
        "p_prompt": nrm((DEPTH, BATCH, SEQ, PLE_DIM), 1.0),
        "p_sample": nrm((DEPTH, DEC_BATCH, DEC_SEQ, PLE_DIM), 1.0),
        "state_wkv": nrm((L, DEC_BATCH, N_HEADS, HEAD_SIZE, HEAD_SIZE), 0.1),
        "state_shift": nrm((L, DEC_BATCH, SHIFT_COLS), 1.0),
        "state_conv": nrm((L, DEC_BATCH, CONV_WIDTH - 1, D), 0.5),
        "norm_ffn1": gain((L, D)),
        "w_ffn1_gate": nrm((L, D, D_FF), D ** -0.5),
        "w_ffn1_up": nrm((L, D, D_FF), D ** -0.5),
        "w_ffn1_down": nrm((L, D_FF, D), D_FF ** -0.5),
        "norm_mix": gain((L, D)),
        "w_in": nrm((L, D, IN_COLS), D ** -0.5),
        "mu_shift": jax.random.uniform(next(ks), (L, SHIFT_COLS), f32),
        "w_decay_up": nrm((L, DECAY_LORA, D), DECAY_LORA ** -0.5),
        "decay_bias": nrm((L, D), 0.5),
        "w_iclr_up": nrm((L, ICLR_LORA, D), ICLR_LORA ** -0.5),
        "iclr_bias": nrm((L, D), 0.1),
        "w_gate_up": nrm((L, GATE_LORA, D), GATE_LORA ** -0.5),
        "k_k": 0.85 + nrm((L, D), 0.05),
        "k_a": gain((L, D)),
        "r_k": nrm((L, N_HEADS, HEAD_SIZE), 0.1),
        "gn_w": gain((L, D)),
        "gn_b": nrm((L, D), 0.02),
        "conv_w": nrm((L, CONV_WIDTH, D), CONV_WIDTH ** -0.5),
        "w_out": nrm((L, D, D), D ** -0.5),
        "norm_ffn2": gain((L, D)),
        "w_ffn2_gate": nrm((L, D, D_FF), D ** -0.5),
        "w_ffn2_up": nrm((L, D, D_FF), D ** -0.5),
        "w_ffn2_down": nrm((L, D_FF, D), D_FF ** -0.5),
        "norm_ple": gain((L, D)),
        "w_ple_gate": nrm((L, D, D), D ** -0.5),
        "w_ple_proj": nrm((L, PLE_DIM, D), PLE_DIM ** -0.5),
        "norm_final": gain((D,)),
    }


def reference(x_prompt, x_sample, p_prompt, p_sample, state_wkv, state_shift, state_conv,
              norm_ffn1, w_ffn1_gate, w_ffn1_up, w_ffn1_down,
              norm_mix, w_in, mu_shift, w_decay_up, decay_bias, w_iclr_up, iclr_bias,
              w_gate_up, k_k, k_a, r_k, gn_w, gn_b, conv_w, w_out,
              norm_ffn2, w_ffn2_gate, w_ffn2_up, w_ffn2_down,
              norm_ple, w_ple_gate, w_ple_proj, norm_final):
    xp, xs = x_prompt, x_sample
    bp = x_prompt.shape[0]
    wkv_p, sh_p, cv_p, wkv_s, sh_s, cv_s = [], [], [], [], [], []
    for i in range(DEPTH):
        lw = (norm_ffn1[i], w_ffn1_gate[i], w_ffn1_up[i], w_ffn1_down[i],
              norm_mix[i], w_in[i], mu_shift[i], w_decay_up[i], decay_bias[i], w_iclr_up[i], iclr_bias[i],
              w_gate_up[i], k_k[i], k_a[i], r_k[i], gn_w[i], gn_b[i], conv_w[i], w_out[i],
              norm_ffn2[i], w_ffn2_gate[i], w_ffn2_up[i], w_ffn2_down[i],
              norm_ple[i], w_ple_gate[i], w_ple_proj[i])
        S0p = jnp.zeros((bp, N_HEADS, HEAD_SIZE, HEAD_SIZE), state_wkv.dtype)
        sh0p = jnp.zeros((bp, SHIFT_COLS), state_shift.dtype)
        cv0p = jnp.zeros((bp, CONV_WIDTH - 1, D_MODEL), state_conv.dtype)
        xp, Sp, shp, cvp = decoder_layer(xp, p_prompt[i], S0p, sh0p, cv0p, *lw)
        xs, Ss, shs, cvs = decoder_layer(xs, p_sample[i], state_wkv[i], state_shift[i], state_conv[i], *lw)
        wkv_p.append(Sp.astype(state_wkv.dtype)); sh_p.append(shp.astype(state_shift.dtype)); cv_p.append(cvp.astype(state_conv.dtype))
        wkv_s.append(Ss.astype(state_wkv.dtype)); sh_s.append(shs.astype(state_shift.dtype)); cv_s.append(cvs.astype(state_conv.dtype))
    y_prompt = rmsnorm(xp, norm_final)
    y_sample = rmsnorm(xs, norm_final)
    new_wkv_prompt = jnp.stack(wkv_p, 0)
    new_shift_prompt = jnp.stack(sh_p, 0)
    new_conv_prompt = jnp.stack(cv_p, 0)
    new_wkv_sample = jnp.stack(wkv_s, 0)
    new_shift_sample = jnp.stack(sh_s, 0)
    new_conv_sample = jnp.stack(cv_s, 0)
    return (y_prompt, y_sample, new_wkv_prompt, new_shift_prompt, new_conv_prompt, new_wkv_sample, new_shift_sample, new_conv_sample)
```

```python
import contextlib
import numpy as np
import concourse.bass as bass
import concourse.mybir as mybir
from concourse.bass_utils import run_bass_kernel_spmd

F32 = mybir.dt.float32
BF16 = mybir.dt.bfloat16
AF = mybir.ActivationFunctionType
ALU = mybir.AluOpType
AX = mybir.AxisListType

D = 1024
KC = 8
DFF = 2816
NF = 22
NP_ = 2048
NS = 16
NTOK = NP_ + NS
NCORE = 8
LAM = float(np.exp(-0.5))
RMS_EPS = 1e-6
GN_EPS = 64e-5
SHIFT_COLS = 3328
IN_COLS = 8448
OFF_R, OFF_K, OFF_V, OFF_L1, OFF_L2 = 0, 1024, 2048, 3072, 3200
OFF_CB, OFF_CC, OFF_CU, OFF_GA, OFF_GB = 3328, 4352, 5376, 6400, 7424
V_NF1, V_NMIX, V_NF2, V_NPLE, V_NFIN, V_DB, V_IB, V_KK, V_KA, V_RK, V_GNW, V_GNB, V_CW, V_MU = (
    0, 8, 16, 24, 32, 40, 48, 56, 64, 72, 80, 88, 96, 120)
NV = 146
NSTG = 2
NW = 9


class _Rec:
    def __init__(self):
        self.call = None

    def __getattr__(self, name):
        def f(*a, **k):
            self.call = (name, a, k)
            return self
        return f


class Sched:
    ENG = ("pe", "act", "dve", "pool", "sp")

    def __init__(self):
        self.ops = {e: [] for e in self.ENG}
        self.count = {e: 0 for e in self.ENG}
        self.waited = {e: {} for e in self.ENG}
        self.last_write = {}
        self.readers = {}
        self.dmacount = {}
        self.out_events = []
        self.pending_barrier = {e: {} for e in self.ENG}
        self.capture = None
        self.opsize = {e: {} for e in self.ENG}
        self.sim_eng = {}
        self.sim_w = {}
        self.sim_r = {}

    def barrier(self):
        snap = {"S_" + e: self.count[e] for e in self.ENG if self.count[e] > 0}
        snap.update(self.dmacount)
        for e in self.ENG:
            for s, v in snap.items():
                if self.pending_barrier[e].get(s, 0) < v:
                    self.pending_barrier[e][s] = v

    def op(self, eng, fn, reads=(), writes=(), dma=None, final=False):
        rec = _Rec()
        fn(rec)
        item = (eng, rec.call, tuple(reads), tuple(writes), dma, final)
        if self.capture is not None:
            self.capture.append(item)
            return None
        return self._op(*item)

    def flush(self, items):
        for it in items:
            self._op(*it)

    def _sim_cost(self, item):
        eng, call, reads, writes, dma, final = item
        cname, cargs, ckw = call
        out = ckw.get("out", cargs[0] if cargs else None)
        try:
            n = 1
            for d in out.shape[1:]:
                n *= int(d)
        except Exception:
            n = 64
        if eng == "pe":
            dur, lat = 0.03 + n * 0.00083, 0.25
        elif eng == "sp":
            dur, lat = 0.05, 3.0
        elif eng == "pool":
            dur, lat = 0.25 + n * 0.0022, 0.2
        else:
            f = 2.0 if cname in ("tensor_tensor", "scalar_tensor_tensor", "tensor_tensor_scan") and not any(w.startswith("B") and w[1:].isdigit() for w in writes) else 1.0
            dur = (0.22 if eng == "act" else 0.08) + n * 0.00105 * f
            if n <= 4:
                dur = 0.3
            lat = 0.15
        t = self.sim_eng.get(eng, 0.0)
        for r in reads:
            tw = self.sim_w.get(r)
            if tw is not None:
                t = max(t, tw[0] + (0.0 if tw[1] == eng else 0.1))
        for w in writes:
            tw = self.sim_w.get(w)
            if tw is not None:
                t = max(t, tw[0] + (0.0 if tw[1] == eng else 0.1))
            tr = self.sim_r.get(w)
            if tr is not None:
                t = max(t, tr)
        return t, dur, lat

    def _sim_commit(self, item):
        eng, call, reads, writes, dma, final = item
        t, dur, lat = self._sim_cost(item)
        self.sim_eng[eng] = t + dur
        end = t + dur + lat
        for w in writes:
            self.sim_w[w] = (end, eng)
            self.sim_r[w] = 0.0
        for r in reads:
            if r not in writes and self.sim_r.get(r, 0.0) < end:
                self.sim_r[r] = end

    def sched_flush(self, items):
        n = len(items)
        preds = [set() for _ in range(n)]
        lastw, readers = {}, {}
        for i, it in enumerate(items):
            eng, call, reads, writes, dma, final = it
            for r in reads:
                if r in lastw:
                    preds[i].add(lastw[r])
            for w in writes:
                if w in lastw:
                    preds[i].add(lastw[w])
                for rr in readers.get(w, ()):
                    preds[i].add(rr)
            for w in writes:
                lastw[w] = i
                readers[w] = []
            for r in reads:
                if r not in writes:
                    readers.setdefault(r, []).append(i)
            if eng == "sp":
                if "sp" in lastw.get("__q", {}):
                    preds[i].add(lastw["__q"]["sp"])
                lastw.setdefault("__q", {})["sp"] = i
        succs = [[] for _ in range(n)]
        for i in range(n):
            preds[i].discard(i)
            for p in preds[i]:
                succs[p].append(i)
        dur = [self._sim_cost(it)[1] + self._sim_cost(it)[2] for it in items]
        blevel = [0.0] * n
        for i in range(n - 1, -1, -1):
            b = 0.0
            for q in succs[i]:
                if blevel[q] > b:
                    b = blevel[q]
            blevel[i] = b + dur[i]
        indeg = [len(p) for p in preds]
        ready = [i for i in range(n) if indeg[i] == 0]
        while ready:
            st = [(self._sim_cost(items[i])[0], i) for i in ready]
            tmin = min(t for t, _ in st)
            cand = [i for t, i in st if t <= tmin + SCHED_SLACK]
            pick = max(cand, key=lambda i: (blevel[i], -i))
            ready.remove(pick)
            self._op(*items[pick])
            for q in succs[pick]:
                indeg[q] -= 1
                if indeg[q] == 0:
                    ready.append(q)

    def merge_flush(self, la, lb):
        i = k = 0
        while i < len(la) or k < len(lb):
            if i >= len(la):
                pick_a = False
            elif k >= len(lb):
                pick_a = True
            else:
                pick_a = self._sim_cost(la[i])[0] + MERGE_BIAS < self._sim_cost(lb[k])[0]
            if pick_a:
                it = la[i]; i += 1
            else:
                it = lb[k]; k += 1
            self._op(*it)

    def _op(self, eng, call, reads, writes, dma, final):
        self._sim_commit((eng, call, reads, writes, dma, final))
        cname, cargs, ckw = call
        out_ = ckw.get("out", cargs[0] if cargs else None)
        try:
            nfree = 1
            for d_ in out_.shape[1:]:
                nfree *= int(d_)
        except Exception:
            nfree = 0
        fn = lambda e, cname=cname, cargs=cargs, ckw=ckw: getattr(e, cname)(*cargs, **ckw)
        waits = {}
        mysem = "S_" + eng

        def need(ev, war=False):
            if ev is None:
                return
            s, v = ev
            if s == mysem:
                if eng in ("pe", "sp"):
                    return
                if not STRICT_SYNC:
                    if war or v <= self.count[eng] - 3:
                        return
                    if min(nfree, self.opsize[eng].get(v, 0)) > SHORT_OP:
                        return
            if self.waited[eng].get(s, 0) >= v:
                return
            if waits.get(s, 0) < v:
                waits[s] = v

        if self.pending_barrier[eng]:
            for s, v in self.pending_barrier[eng].items():
                need((s, v))
            self.pending_barrier[eng] = {}
        for r in reads:
            need(self.last_write.get(r))
        for w in writes:
            need(self.last_write.get(w))
            for s, v in self.readers.get(w, {}).items():
                need((s, v), war=True)
        if dma is not None:
            self.dmacount[dma] = self.dmacount.get(dma, 0) + 16
            ev = (dma, self.dmacount[dma])
        else:
            self.count[eng] += 1
            ev = (mysem, self.count[eng])
            self.opsize[eng][self.count[eng]] = nfree
        for w in writes:
            self.last_write[w] = ev
            self.readers[w] = {}
        for r in reads:
            if r in writes:
                continue
            d = self.readers.setdefault(r, {})
            if d.get(ev[0], 0) < ev[1]:
                d[ev[0]] = ev[1]
        for s, v in waits.items():
            self.waited[eng][s] = v
        self.ops[eng].append((list(waits.items()), fn, ev, dma is not None))
        if final:
            self.out_events.append(ev)
        return ev

    def emit(self, nc, stack):
        names = ["S_" + e for e in self.ENG] + sorted(self.dmacount.keys())
        sems = {n: stack.enter_context(nc.semaphore(n)) for n in names}
        block = stack.enter_context(nc.Block())
        fin = {}
        for s, v in self.out_events:
            fin[s] = max(fin.get(s, 0), v)

        def replay(engname):
            def body(e):
                for waits, fn, ev, isdma in self.ops[engname]:
                    for s, v in waits:
                        e.wait_ge(sems[s], v)
                    fn(e).then_inc(sems[ev[0]], 16 if isdma else 1)
                if engname == "sp":
                    for s, v in fin.items():
                        e.wait_ge(sems[s], v)
            return body

        block.tensor(replay("pe"))
        block.scalar(replay("act"))
        block.vector(replay("dve"))
        block.gpsimd(replay("pool"))
        block.sync(replay("sp"))


DEBUG = False
INTERLEAVE = True
SCHED_SLACK = 0.3
LIST_SCHED = True
STRICT_SYNC = True
WIN_STEPS = 3
SHORT_OP = 64
MERGE_BIAS = -1.0
DBG_NAMES = []
DBG_RESULTS = {}


def build_program():
    nc = bass.Bass("TRN2", target_bir_lowering=False)
    S = Sched()
    stack = contextlib.ExitStack()
    del DBG_NAMES[:]

    def dbg(name, ap, res, shape, dt=F32):
        if not DEBUG:
            return
        d = nc.dram_tensor("dbg_" + name, list(shape), dt, kind="ExternalOutput").ap()
        DBG_NAMES.append("dbg_" + name)
        idx = tuple(slice(None) for _ in shape)
        S.op("sp", lambda e: e.dma_start(out=d[idx], in_=ap), reads=list(res), dma="D_dbg_" + name, final=True)

    def din(name, shape):
        return nc.dram_tensor(name, list(shape), F32, kind="ExternalInput").ap()

    def dout(name, shape):
        return nc.dram_tensor(name, list(shape), F32, kind="ExternalOutput").ap()

    xT_d = din("xT", [D, NTOK])
    pT_d = din("pT", [256, NTOK])
    vec_d = din("vecs", [128, NV])
    shiftS_d = din("shiftS", [128, 26 * NS])
    convS_d = din("convS", [128, KC * 2 * NS])
    wkvS_d = din("wkvS", [KC, 128, NS * 64])
    w_f1g = din("w_f1g", [D, DFF]); w_f1u = din("w_f1u", [D, DFF]); w_f1d = din("w_f1d", [DFF, D])
    w_f2g = din("w_f2g", [D, DFF]); w_f2u = din("w_f2u", [D, DFF]); w_f2d = din("w_f2d", [DFF, D])
    w_in = din("w_in", [D, IN_COLS])
    w_du = din("w_du", [64, D]); w_iu = din("w_iu", [64, D]); w_gu = din("w_gu", [128, D])
    w_out = din("w_out", [D, D]); w_pg = din("w_pg", [D, D]); w_pp = din("w_pp", [256, D])
    yT_d = dout("yT", [D, NTOK])
    wkvP_d = dout("wkvP", [KC, 128, 64])
    shout_d = dout("shout", [128, 26 * 17])
    convout_d = dout("convout", [128, KC * 2 * 17])
    wkvSo_d = dout("wkvSo", [KC, 128, NS * 64])

    def sb(name, shape, dt=F32):
        return stack.enter_context(nc.sbuf_tensor(name, list(shape), dt))

    with stack:
        xT = sb("xT_sb", [128, KC, NTOK])
        vec = sb("vec", [128, NV])
        ommu = sb("ommu", [128, 26])
        omka = sb("omka", [128, KC])
        hbias = sb("hbias", [128, 2 * KC])
        ident = sb("ident", [128, 128], BF16)
        ones_bf = sb("ones_bf", [128, 128], BF16)
        bd1 = sb("bd1", [128, 128])
        bdm = sb("bdm", [128, 128])
        bdrk = sb("bdrk", [128, 128])
        identH = sb("identH", [128, 64])
        mask4 = sb("mask4", [128, 512], BF16)
        maskL = sb("maskL", [128, 128], BF16)
        resetm = sb("resetm", [128, 256], BF16)
        stg = sb("stg", [128, NSTG, 1024])
        wring = sb("wring", [128, NW, 1024], BF16)
        carr = sb("carr", [128, 26])
        shout = sb("shout_sb", [128, 26, 17])
        shiftS = sb("shiftS_sb", [128, 26, NS])
        convS = sb("convS_sb", [128, KC, 2, NS])
        convout = sb("convout_sb", [128, KC, 2, 17])
        ccar = sb("ccar", [128, KC, 2])
        Hst = sb("Hst", [128, KC, 128])
        ARENA = 26328
        arena = sb("arena", [128, ARENA])

        def carve32(off, n):
            return arena[:, off:off + n]

        def carve16(off, n):
            return arena[:, off:off + n // 2].bitcast(BF16)

        banks = [stack.enter_context(nc.psum_tensor("bank%d" % i, [128, 512], F32)) for i in range(8)]

        cnt = {"stg": 0, "w": 0}
        wgen = [0] * NW

        def dve(fn, r=(), w=()):
            S.op("dve", fn, reads=r, writes=w)

        def act(fn, r=(), w=()):
            S.op("act", fn, reads=r, writes=w)

        def pool(fn, r=(), w=()):
            S.op("pool", fn, reads=r, writes=w)

        def mm(bank, out_ap, lhsT, rhs, start, stop, r):
            S.op("pe", lambda e: e.matmul(out_ap, lhsT, rhs, start=start, stop=stop, skip_group_check=True),
                 reads=r, writes=["B%d" % bank])

        def dma_in(out_ap, in_ap, res, key):
            S.op("sp", lambda e: e.dma_start(out=out_ap, in_=in_ap), writes=([res] if isinstance(res, str) else list(res)), dma=key)

        def dma_out(out_ap, in_ap, res, key):
            S.op("sp", lambda e: e.dma_start(out=out_ap, in_=in_ap), reads=([res] if isinstance(res, str) else list(res)), dma=key, final=True)

        def stage_load(src_ap, view_fn, prt=(0, 128)):
            si = cnt["stg"] % NSTG
            cnt["stg"] += 1
            flat = stg[prt[0]:prt[1], si, :]
            S.op("sp", lambda e: e.dma_start(out=view_fn(flat), in_=src_ap), writes=["stg%d" % si], dma="D_stg%d" % si)
            return flat, "stg%d" % si

        def wload(src_ap, kind, slot=None):
            if slot is None:
                wi = cnt["w"] % NW
                cnt["w"] += 1
            else:
                wi = slot
            wgen[wi] += 1
            if kind == "k8":
                vf = lambda a: a.rearrange("p (k c) -> p k c", c=128)
                src = src_ap.rearrange("(k p) c -> p k c", p=128)
            elif kind == "k2":
                vf = lambda a: a[:, 0:256].rearrange("p (k c) -> p k c", c=128)
                src = src_ap.rearrange("(k p) c -> p k c", p=128)
            else:
                vf = lambda a: a
                src = src_ap
            flat, sres = stage_load(src, vf)
            dst = wring[:, wi, :]
            n = 256 if kind == "k2" else 1024
            pool(lambda e: e.tensor_copy(dst[:, 0:n], flat[:, 0:n]), r=[sres], w=["w%d" % wi])
            return vf(dst), "w%d" % wi

        TILES = [(0, 512), (512, 512), (1024, 512), (1536, 512), (2048, 16)]
        FT = [(0, 413), (413, 413), (826, 413), (1239, 413), (1652, 412)]

        dma_in(vec[:], vec_d[:, :], "vec", "D_vec")
        for kc in range(KC):
            dma_in(xT[:, kc, :], xT_d[kc * 128:(kc + 1) * 128, :], "x%d" % kc, "D_x%d" % kc)
        dma_in(shiftS[:].rearrange("p a b -> p (a b)"), shiftS_d[:, :], "shiftS", "D_shiftS")
        dma_in(convS[:].rearrange("p a b c -> p (a b c)"), convS_d[:, :], "convS", "D_convS")
        pool(lambda e: e.memset(ident[:], 0.0), w=["ident"])
        pool(lambda e: e.affine_select(ident[:], ident[:], [[-1, 128]], ALU.not_equal, 1.0, base=0, channel_multiplier=1),
             r=["ident"], w=["ident"])
        pool(lambda e: e.memset(ones_bf[:], 1.0), w=["ones_bf"])
        pool(lambda e: e.memset(bd1[:], 0.0), w=["bd1"])
        pool(lambda e: e.memset(bd1[0:64, 0:64], 1.0), w=["bd1"])
        pool(lambda e: e.memset(bd1[64:128, 64:128], 1.0), w=["bd1"])
        pool(lambda e: e.tensor_scalar(bdm[:], bd1[:], 1.0 / 64.0, None, ALU.mult), r=["bd1"], w=["bdm"])
        pool(lambda e: e.tensor_copy(identH[0:64, :], ident[0:64, 0:64]), r=["ident"], w=["identH"])
        pool(lambda e: e.tensor_copy(identH[64:128, :], ident[64:128, 64:128]), r=["ident"], w=["identH"])
        pool(lambda e: e.memset(mask4[:], 1.0), w=["mask4"])
        for b in range(4):
            base = -1 if b < 2 else 0
            blk = mask4[:, b * 128:(b + 1) * 128]
            pool(lambda e, blk=blk, base=base: e.affine_select(blk, blk, [[1, 128]], ALU.is_ge, 0.0, base=base, channel_multiplier=-1),
                 r=["mask4"], w=["mask4"])
        pool(lambda e: e.memset(maskL[:], 1.0), w=["maskL"])
        pool(lambda e: e.affine_select(maskL[:], maskL[:], [[-1, 128]], ALU.is_ge, 0.0, base=-1, channel_multiplier=1),
             r=["maskL"], w=["maskL"])
        pool(lambda e: e.memset(resetm[:], 1.0), w=["resetm"])
        pool(lambda e: e.memset(resetm[:].rearrange("p (c t) -> p c t", t=128)[:, :, 0:1], 0.0), w=["resetm"])
        pool(lambda e: e.memset(carr[:], 0.0), w=["carr"])
        pool(lambda e: e.memset(ccar[:], 0.0), w=["ccar"])
        pool(lambda e: e.memset(Hst[:], 0.0), w=["Hst"])
        dve(lambda e: e.tensor_scalar(ommu[:], vec[:, V_MU:V_MU + 26], -1.0, 1.0, ALU.mult, ALU.add), r=["vec"], w=["ommu"])
        dve(lambda e: e.tensor_scalar(omka[:], vec[:, V_KA:V_KA + 8], -1.0, 1.0, ALU.mult, ALU.add), r=["vec"], w=["omka"])
        dve(lambda e: e.tensor_scalar(hbias[:, 0:KC], vec[:, V_IB:V_IB + 8], 0.5, None, ALU.mult), r=["vec"], w=["hbias"])
        dve(lambda e: e.tensor_scalar(hbias[:, KC:2 * KC], vec[:, V_DB:V_DB + 8], 0.5, None, ALU.mult), r=["vec"], w=["hbias"])
        consts = ["vec", "ommu", "omka"]

        def sigmoid(out, in_, r, w, bias=None, eng_fix="dve"):
            if bias is None:
                act(lambda e: e.activation(out, in_, AF.Tanh, scale=0.5), r=r, w=w)
            else:
                act(lambda e: e.activation(out, in_, AF.Tanh, scale=0.5, bias=bias), r=r, w=w)
            wo = [x for x in w if not x.startswith("B")]
            dve(lambda e: e.tensor_scalar(out, out, 0.5, 0.5, ALU.mult, ALU.add), r=wo, w=wo)


        def rmsnorm(vcol, tiles, hdst, hres, sq_ap, rs_ap, inplace=False, rsres="rs"):
            for ti, (t0, n) in enumerate(tiles):
                for kc in range(KC):
                    sq = sq_ap[:, kc % 2, 0:n]
                    act(lambda e, sq=sq, kc=kc: e.activation(sq, xT[:, kc, t0:t0 + n], AF.Square),
                        r=["x%d" % kc], w=["sq%d" % (kc % 2)])
                    mm(2, banks[2][:, 0:n], ones_bf[:], sq, kc == 0, kc == KC - 1, ["ones_bf", "sq%d" % (kc % 2)])
                rs = rs_ap[:, 0:n]
                dve(lambda e, rs=rs, n=n: e.tensor_scalar(rs, banks[2][:, 0:n], 1.0 / D, RMS_EPS, ALU.mult, ALU.add),
                    r=["B2"], w=[rsres, "B2"])
                act(lambda e, rs=rs: e.activation(rs, rs, AF.Ln), r=[rsres], w=[rsres])
                act(lambda e, rs=rs: e.activation(rs, rs, AF.Exp, scale=-0.5), r=[rsres], w=[rsres])
                for kc in range(KC):
                    if inplace:
                        o = xT[:, kc, t0:t0 + n]
                        dve(lambda e, o=o, kc=kc, rs=rs: e.scalar_tensor_tensor(o, o, vec[:, vcol + kc:vcol + kc + 1], rs, ALU.mult, ALU.mult),
                            r=[rsres, "vec"], w=["x%d" % kc])
                    else:
                        lt0 = t0 - tiles[0][0]
                        o = hdst[:, kc, lt0:lt0 + n]
                        dve(lambda e, o=o, kc=kc, rs=rs: e.scalar_tensor_tensor(o, xT[:, kc, t0:t0 + n], vec[:, vcol + kc:vcol + kc + 1], rs, ALU.mult, ALU.mult),
                            r=[rsres, "vec", "x%d" % kc], w=[hres])

        def ffn(wg_d, wu_d, wd_d, vcol):
            S.barrier()
            hT = carve16(0, KC * NTOK).rearrange("p (k t) -> p k t", t=NTOK)
            actb = carve16(8256, 4 * NTOK).rearrange("p (f t) -> p f t", t=NTOK)
            sq_ap = carve16(12384, 1024).rearrange("p (a t) -> p a t", t=512)
            rs_ap = carve32(12896, 512)
            sil = carve32(13408, 1024).rearrange("p (a t) -> p a t", t=512)
            rmsnorm(vcol, FT, hT, "hT", sq_ap, rs_ap)
            F1 = (vcol == V_NF1)
            if F1:
                dbg("f_h", hT[:, 0, 0:1024], ["hT"], [128, 1024], BF16)
                dbg("f_rs", rs_ap, ["rs"], [128, 512])
            groups = [[0, 1, 2, 3], [4, 5, 6, 7], [8, 9, 10, 11], [12, 13, 14, 15], [16, 17, 18], [19, 20, 21]]
            it = 0
            for grp in groups:
                for fl, f in enumerate(grp):
                    wg, wgr = wload(wg_d[:, f * 128:(f + 1) * 128], "k8")
                    wu, wur = wload(wu_d[:, f * 128:(f + 1) * 128], "k8")
                    for (t0, n) in FT:
                        bg = (it % 2) * 2
                        bu = bg + 1
                        it += 1
                        for kc in range(KC):
                            mm(bg, banks[bg][:, 0:n], wg[:, kc, :], hT[:, kc, t0:t0 + n], kc == 0, kc == KC - 1, [wgr, "hT"])
                        for kc in range(KC):
                            mm(bu, banks[bu][:, 0:n], wu[:, kc, :], hT[:, kc, t0:t0 + n], kc == 0, kc == KC - 1, [wur, "hT"])
                        sl = sil[:, bg // 2, 0:n]
                        act(lambda e, sl=sl, bg=bg, n=n: e.activation(sl, banks[bg][:, 0:n], AF.Silu),
                            r=["B%d" % bg], w=["sil%d" % (bg // 2), "B%d" % bg])
                        o = actb[:, fl, t0:t0 + n]
                        dve(lambda e, o=o, sl=sl, bu=bu, n=n: e.tensor_tensor(o, banks[bu][:, 0:n], sl, ALU.mult),
                            r=["sil%d" % (bg // 2), "B%d" % bu], w=["act%d" % fl, "B%d" % bu])
                if F1 and grp[0] == 0:
                    dbg("f_act0", actb[:, 0, 0:1024], ["act0"], [128, 1024], BF16)
                    dbg("f_act3", actb[:, 3, 0:1024], ["act3"], [128, 1024], BF16)
                wds = [wload(wd_d[f * 128:(f + 1) * 128, :], "row") for f in grp]
                for oc in range(KC):
                    for (t0, n) in FT:
                        bk = 4 + (it % 2)
                        it += 1
                        for fl, f in enumerate(grp):
                            mm(bk, banks[bk][:, 0:n], wds[fl][0][:, oc * 128:(oc + 1) * 128], actb[:, fl, t0:t0 + n],
                               fl == 0, fl == len(grp) - 1, [wds[fl][1], "act%d" % fl])
                        o = xT[:, oc, t0:t0 + n]
                        dve(lambda e, o=o, bk=bk, n=n: e.scalar_tensor_tensor(o, banks[bk][:, 0:n], 0.5, o, ALU.mult, ALU.add),
                            r=["B%d" % bk], w=["x%d" % oc, "B%d" % bk])

        def mixer():
            S.barrier()
            HW_ = 1040
            NM = 256
            o = 0
            hTh = carve16(o, KC * HW_).rearrange("p (k t) -> p k t", t=HW_); o += KC * HW_ // 2
            lin1 = carve16(o, HW_); o += HW_ // 2
            lin2 = carve16(o, HW_); o += HW_ // 2
            LUd = carve16(o, 1024); o += 512
            LUi = carve16(o, 1024); o += 512
            WGU = carve16(o, 1024); o += 512
            nscr = carve32(o, 1536).rearrange("p (a t) -> p a t", t=512); o += 1536
            sq_ap = carve16(o, 1024).rearrange("p (a t) -> p a t", t=512); o += 512
            SJ = carve32(o, 1024); o += 1024
            PP = carve32(o, 1024); o += 1024
            RHSb = carve16(o, 128); o += 64
            HBD = carve16(o, 128); o += 64
            NT_ = 12
            BUF = []
            for s_ in range(2):
                b = {}
                b["Tp"] = carve32(o, NT_ * NM).rearrange("p (a t) -> p a t", t=NM); o += NT_ * NM
                b["zraw"] = carve32(o, NM); o += NM
                b["ztmp"] = carve32(o, NM); o += NM
                b["CUE"] = carve32(o, NM + 2); o += NM + 2
                for nm in ("KT", "BT", "KTP", "BTP", "KKT", "RT", "VBF", "MG"):
                    b[nm] = carve16(o, NM); o += NM // 2
                b["RK"] = carve16(o, 4 * NM).rearrange("p (c s t) -> p c s t", s=4, t=128); o += 2 * NM
                b["BP"] = carve16(o, 2 * NM).rearrange("p (c s t) -> p c s t", s=2, t=128); o += NM
                b["WC"] = carve32(o, 2); o += 2
                BUF.append(b)
            TOK = carve16(o, 2048).rearrange("p (b k s t) -> p b k s t", k=4, s=2, t=128); o += 1024
            S1 = carve16(o, 1024).rearrange("p (b t) -> p b t", t=512); o += 512
            S2 = carve16(o, 1024).rearrange("p (b t) -> p b t", t=512); o += 512
            LMb = carve16(o, 512).rearrange("p (b t) -> p b t", t=256); o += 256
            MMb = carve16(o, 2048).rearrange("p (c b t) -> p c b t", b=2, t=512); o += 1024
            TTb = carve16(o, 1024).rearrange("p (c b t) -> p c b t", b=2, t=256); o += 512
            TTF = carve16(o, 512).rearrange("p (b t) -> p b t", t=256); o += 256
            assert o <= ARENA, o

            pool(lambda e: e.memset(LUd[:, :], 0.0), w=["LUd"])
            pool(lambda e: e.memset(LUi[:, :], 0.0), w=["LUi"])
            pool(lambda e: e.memset(TOK[:].rearrange("p b k s t -> p (b k s t)"), 0.0), w=["TOK0", "TOK1"])
            for s_ in range(2):
                pool(lambda e: e.memset(BUF[s_]["RK"][:].rearrange("p c s t -> p (c s t)"), 0.0), w=["RK_%d" % s_])
                pool(lambda e: e.memset(BUF[s_]["BP"][:].rearrange("p c s t -> p (c s t)"), 0.0), w=["BP_%d" % s_])
            fl_, sr = stage_load(w_du[:, :], lambda a: a, prt=(0, 64))
            pool(lambda e: e.tensor_copy(LUd[0:64, :], fl_), r=[sr], w=["LUd"])
            fl_, sr = stage_load(w_iu[:, :], lambda a: a, prt=(64, 128))
            pool(lambda e: e.tensor_copy(LUi[64:128, :], fl_), r=[sr], w=["LUi"])
            fl_, sr = stage_load(w_gu[:, :], lambda a: a)
            pool(lambda e: e.tensor_copy(WGU[:, :], fl_), r=[sr], w=["WGU"])

            def proj(bank, wt, wres, tiles0, t0, n):
                lt0 = t0 - tiles0
                for kc in range(KC):
                    mm(bank, banks[bank][:, 0:n], wt[:, kc, :], hTh[:, kc, lt0:lt0 + n], kc == 0, kc == KC - 1, [wres, "hTh"])

            def shift_evac(bank, q, n, is_sample, dst, dres, zraw, ztmp, zr, zt):
                bk = "B%d" % bank
                act(lambda e: e.activation(zraw[:, 0:n], banks[bank][:, 0:n], AF.Copy), r=[bk], w=[zr, bk])
                act(lambda e: e.activation(ztmp[:, 0:n], banks[bank][:, 0:n], AF.Identity, scale=ommu[:, q:q + 1]),
                    r=[bk, "ommu"], w=[zt, bk])
                mu = vec[:, V_MU + q:V_MU + q + 1]
                if not is_sample:
                    dve(lambda e: e.scalar_tensor_tensor(dst[:, 1:n], zraw[:, 0:n - 1], mu, ztmp[:, 1:n], ALU.mult, ALU.add),
                        r=[zr, zt, "vec"], w=[dres])
                    dve(lambda e: e.scalar_tensor_tensor(dst[:, 0:1], carr[:, q:q + 1], mu, ztmp[:, 0:1], ALU.mult, ALU.add),
                        r=["carr%d" % q, zt, "vec"], w=[dres])
                    dve(lambda e: e.tensor_copy(carr[:, q:q + 1], zraw[:, n - 1:n]), r=[zr], w=["carr%d" % q])
                else:
                    dve(lambda e: e.scalar_tensor_tensor(dst[:, 0:n], shiftS[:, q, :], mu, ztmp[:, 0:n], ALU.mult, ALU.add),
                        r=["shiftS", zt, "vec"], w=[dres])
                    pool(lambda e: e.tensor_copy(shout[:, q, 1:17], zraw[:, 0:n]), r=[zr], w=["shout"])

            def phase_A(j, t0, n, tiles0, sx, W, first_tile):
                b = BUF[sx]
                Tp = b["Tp"]
                rn = lambda nm: "%s_%d" % (nm, sx)
                T = lambda i, n_=n: Tp[:, i, 0:n_]
                lt0 = t0 - tiles0
                smp = (n == NS)
                jc = slice(j * 128, (j + 1) * 128)
                if first_tile:
                    dve(lambda e: e.tensor_scalar(bdrk[:], bd1[:], vec[:, V_RK + j:V_RK + j + 1], None, ALU.mult), r=["bd1", "vec"], w=["bdrk"])
                R, K_, V_, A_, SG, G_ = T(0), T(1), T(2), T(3), T(4), T(5)
                zz = (b["zraw"], b["ztmp"], rn("zraw"), rn("ztmp"))
                proj(0, W["r"][0], W["r"][1], tiles0, t0, n)
                shift_evac(0, j, n, smp, R, rn("T0"), *zz)
                proj(1, W["k"][0], W["k"][1], tiles0, t0, n)
                shift_evac(1, 8 + j, n, smp, K_, rn("T1"), *zz)
                proj(0, W["v"][0], W["v"][1], tiles0, t0, n)
                shift_evac(0, 16 + j, n, smp, V_, rn("T2"), *zz)
                mm(1, banks[1][:, 0:n], LUi[:, jc], lin1[:, lt0:lt0 + n], True, True, ["LUi", "lin1"])
                sigmoid(A_, banks[1][:, 0:n], ["B1", "hbias"], [rn("T3"), "B1"], bias=hbias[:, j:j + 1])
                mm(0, banks[0][:, 0:n], LUd[:, jc], lin1[:, lt0:lt0 + n], True, True, ["LUd", "lin1"])
                sigmoid(SG, banks[0][:, 0:n], ["B0", "hbias"], [rn("T4"), "B0"], bias=hbias[:, KC + j:KC + j + 1])
                mm(1, banks[1][:, 0:n], WGU[:, jc], lin2[:, lt0:lt0 + n], True, True, ["WGU", "lin2"])
                act(lambda e: e.activation(G_, banks[1][:, 0:n], AF.Copy), r=["B1"], w=[rn("T5"), "B1"])
                KKN, KM, RI = T(10), T(11), T(9)
                act(lambda e: e.activation(KKN, K_, AF.Copy, scale=vec[:, V_KK + j:V_KK + j + 1]), r=[rn("T1"), "vec"], w=[rn("T10")])
                act(lambda e: e.activation(RI, KKN, AF.Square), r=[rn("T10")], w=[rn("T9")])
                mm(2, banks[2][:, 0:n], bd1[:], RI, True, True, ["bd1", rn("T9")])
                dve(lambda e: e.tensor_scalar(RI, banks[2][:, 0:n], 1e-24, None, ALU.max), r=["B2"], w=[rn("T9"), "B2"])
                act(lambda e: e.activation(RI, RI, AF.Ln), r=[rn("T9")], w=[rn("T9")])
                act(lambda e: e.activation(RI, RI, AF.Exp, scale=-0.5), r=[rn("T9")], w=[rn("T9")])
                pool(lambda e: e.tensor_tensor(KKN, KKN, RI, ALU.mult), r=[rn("T10"), rn("T9")], w=[rn("T10")])
                dve(lambda e: e.tensor_scalar(KM, A_, vec[:, V_KA + j:V_KA + j + 1], omka[:, j:j + 1], ALU.mult, ALU.add),
                    r=[rn("T3"), "vec", "omka"], w=[rn("T11")])
                pool(lambda e: e.tensor_tensor(KM, KM, K_, ALU.mult), r=[rn("T11"), rn("T1")], w=[rn("T11")])
                BON = T(1)
                pool(lambda e: e.tensor_tensor(BON, R, KM, ALU.mult), r=[rn("T0"), rn("T11")], w=[rn("T1")])
                mm(2, banks[2][:, 0:n], bdrk[:], BON, True, True, ["bdrk", rn("T1")])
                dve(lambda e: e.tensor_tensor(BON, banks[2][:, 0:n], V_, ALU.mult), r=["B2", rn("T2")], w=[rn("T1"), "B2"])
                B_ = T(3)
                pool(lambda e: e.tensor_tensor(B_, B_, KKN, ALU.mult), r=[rn("T3"), rn("T10")], w=[rn("T3")])
                if smp:
                    return
                NCH = n // 128
                KT, BT, KTP, BTP, KKT, RT, VBF, RK, BP, WC = (b[k_] for k_ in ("KT", "BT", "KTP", "BTP", "KKT", "RT", "VBF", "RK", "BP", "WC"))
                CUM, E1, E2, E3 = T(6), T(7), T(8), T(9)
                dve(lambda e: e.tensor_tensor_scan(CUM, resetm[:, 0:n], SG, 0.0, ALU.mult, ALU.add), r=["resetm", rn("T4")], w=[rn("T6")])
                act(lambda e: e.activation(E1, CUM, AF.Exp, scale=-LAM), r=[rn("T6")], w=[rn("T7")])
                act(lambda e: e.activation(E2, CUM, AF.Exp, scale=LAM), r=[rn("T6")], w=[rn("T8")])
                pool(lambda e: e.tensor_tensor(E3, CUM, SG, ALU.subtract), r=[rn("T6"), rn("T4")], w=[rn("T9")])
                act(lambda e: e.activation(E3, E3, AF.Exp, scale=-LAM), r=[rn("T9")], w=[rn("T9")])
                KF, BF = T(6), T(4)
                pool(lambda e: e.tensor_tensor(KF, KM, E2, ALU.mult), r=[rn("T11"), rn("T8")], w=[rn("T6")])
                pool(lambda e: e.tensor_tensor(BF, B_, E2, ALU.mult), r=[rn("T3"), rn("T8")], w=[rn("T4")])
                act(lambda e: e.activation(KT[:, 0:n], KF, AF.Copy), r=[rn("T6")], w=[rn("KT")])
                act(lambda e: e.activation(BT[:, 0:n], BF, AF.Copy), r=[rn("T4")], w=[rn("BT")])
                v3 = lambda a: a.rearrange("p (c t) -> p c t", t=128)
                v3E1 = v3(E1)
                dve(lambda e: e.tensor_copy(WC[:, 0:NCH], v3E1[:, :, 127]), r=[rn("T7")], w=[rn("WC")])
                WCb = v3E1[:, :, 127:128].broadcast_to([128, NCH, 128])
                dve(lambda e: e.tensor_tensor(v3(KTP[:, 0:n]), v3(KF), WCb, ALU.mult), r=[rn("T6"), rn("T7")], w=[rn("KTP")])
                dve(lambda e: e.tensor_tensor(v3(BTP[:, 0:n]), v3(BF), WCb, ALU.mult), r=[rn("T4"), rn("T7")], w=[rn("BTP")])
                pool(lambda e: e.tensor_tensor(KKT[:, 0:n], KKN, E3, ALU.mult), r=[rn("T10"), rn("T9")], w=[rn("KKT")])
                pool(lambda e: e.tensor_tensor(RT[:, 0:n], R, E1, ALU.mult), r=[rn("T0"), rn("T7")], w=[rn("RT")])
                act(lambda e: e.activation(VBF[:, 0:n], V_, AF.Copy), r=[rn("T2")], w=[rn("VBF")])
                pool(lambda e: e.tensor_copy(RK[0:64, :, 0, :], v3(KKT[0:64, 0:n])), r=[rn("KKT")], w=[rn("RK")])
                act(lambda e: e.activation(RK[64:128, :, 1, :], v3(KKT[64:128, 0:n]), AF.Copy), r=[rn("KKT")], w=[rn("RK")])
                pool(lambda e: e.tensor_copy(RK[0:64, :, 2, :], v3(RT[0:64, 0:n])), r=[rn("RT")], w=[rn("RK")])
                act(lambda e: e.activation(RK[64:128, :, 3, :], v3(RT[64:128, 0:n]), AF.Copy), r=[rn("RT")], w=[rn("RK")])
                pool(lambda e: e.tensor_copy(BP[0:64, :, 0, :], v3(BT[0:64, 0:n])), r=[rn("BT")], w=[rn("BP")])
                act(lambda e: e.activation(BP[64:128, :, 1, :], v3(BT[64:128, 0:n]), AF.Copy), r=[rn("BT")], w=[rn("BP")])

            def phase_B(j, t0, n, tiles0, sx, W, first_tile, last_unit_of_j):
                b = BUF[sx]
                Tp = b["Tp"]
                rn = lambda nm: "%s_%d" % (nm, sx)
                T = lambda i, n_=n: Tp[:, i, 0:n_]
                smp = (n == NS)
                H32 = Hst[:, j, :]
                R, V_, SG, G_, KKN, KM, B_, BON = T(0), T(2), T(4), T(5), T(10), T(11), T(3), T(1)
                Y_ = T(8)
                if first_tile:
                    act(lambda e: e.activation(HBD[:, :], H32, AF.Copy), r=["Hst"], w=["HBD"])
                if not smp:
                    wkv_chunks(j, n, sx, b, Y_, rn, H32)
                else:
                    wkv_sample(j, n, R, V_, SG, KKN, KM, B_, Y_, rn)
                if last_unit_of_j:
                    dma_out(wkvP_d[j, 0:64, :], Hst[0:64, j, 0:64], "Hst", "D_o_wkvPa%d" % j)
                    dma_out(wkvP_d[j, 64:128, :], Hst[64:128, j, 64:128], "Hst", "D_o_wkvPb%d" % j)
                S.capture.append("SPLIT")
                YC, SQ = T(7), T(9)
                mm(2, banks[2][:, 0:n], bdm[:], Y_, True, True, ["bdm", rn("T8")])
                dve(lambda e: e.tensor_tensor(YC, Y_, banks[2][:, 0:n], ALU.subtract), r=[rn("T8"), "B2"], w=[rn("T7"), "B2"])
                act(lambda e: e.activation(SQ, YC, AF.Square), r=[rn("T7")], w=[rn("T9")])
                mm(2, banks[2][:, 0:n], bdm[:], SQ, True, True, ["bdm", rn("T9")])
                dve(lambda e: e.tensor_scalar(SQ, banks[2][:, 0:n], GN_EPS, None, ALU.add), r=["B2"], w=[rn("T9"), "B2"])
                act(lambda e: e.activation(SQ, SQ, AF.Ln), r=[rn("T9")], w=[rn("T9")])
                act(lambda e: e.activation(SQ, SQ, AF.Exp, scale=-0.5), r=[rn("T9")], w=[rn("T9")])
                dve(lambda e: e.tensor_tensor(YC, YC, SQ, ALU.mult), r=[rn("T7"), rn("T9")], w=[rn("T7")])
                act(lambda e: e.activation(YC, YC, AF.Identity, scale=vec[:, V_GNW + j:V_GNW + j + 1], bias=vec[:, V_GNB + j:V_GNB + j + 1]),
                    r=[rn("T7"), "vec"], w=[rn("T7")])
                pool(lambda e: e.tensor_tensor(YC, YC, BON, ALU.add), r=[rn("T7"), rn("T1")], w=[rn("T7")])
                pool(lambda e: e.tensor_tensor(YC, YC, G_, ALU.mult), r=[rn("T7"), rn("T5")], w=[rn("T7")])
                proj(0, W["ga"][0], W["ga"][1], tiles0, t0, n)
                sigmoid(SQ, banks[0][:, 0:n], ["B0"], [rn("T9"), "B0"])
                dve(lambda e: e.tensor_tensor(YC, YC, SQ, ALU.mult), r=[rn("T7"), rn("T9")], w=[rn("T7")])
                UU, Y1 = T(6), T(10)
                CUE, MG = b["CUE"], b["MG"]
                proj(1, W["cu"][0], W["cu"][1], tiles0, t0, n)
                act(lambda e: e.activation(UU, banks[1][:, 0:n], AF.Copy), r=["B1"], w=[rn("T6"), "B1"])
                proj(0, W["cc"][0], W["cc"][1], tiles0, t0, n)
                cw = lambda i: vec[:, V_CW + 8 * i + j:V_CW + 8 * i + j + 1]
                if not smp:
                    dve(lambda e: e.tensor_copy(CUE[:, 0:2], ccar[:, j, :]), r=["ccar"], w=[rn("CUE")])
                    dve(lambda e: e.tensor_tensor(CUE[:, 2:2 + n], banks[0][:, 0:n], UU, ALU.mult), r=["B0", rn("T6")], w=[rn("CUE"), "B0"])
                    dve(lambda e: e.tensor_copy(ccar[:, j, :], CUE[:, n:n + 2]), r=[rn("CUE")], w=["ccar"])
                    act(lambda e: e.activation(Y1, CUE[:, 0:n], AF.Copy, scale=cw(0)), r=[rn("CUE"), "vec"], w=[rn("T10")])
                    dve(lambda e: e.scalar_tensor_tensor(Y1, CUE[:, 1:n + 1], cw(1), Y1, ALU.mult, ALU.add), r=[rn("CUE"), "vec", rn("T10")], w=[rn("T10")])
                    dve(lambda e: e.scalar_tensor_tensor(Y1, CUE[:, 2:n + 2], cw(2), Y1, ALU.mult, ALU.add), r=[rn("CUE"), "vec", rn("T10")], w=[rn("T10")])
                    if t0 + n == NP_:
                        pool(lambda e: e.tensor_copy(convout[:, j, :, 0], CUE[:, n:n + 2]), r=[rn("CUE")], w=["convout"])
                else:
                    dve(lambda e: e.tensor_tensor(CUE[:, 0:n], banks[0][:, 0:n], UU, ALU.mult), r=["B0", rn("T6")], w=[rn("CUE"), "B0"])
                    dve(lambda e: e.tensor_scalar(Y1, convS[:, j, 0, :], cw(0), None, ALU.mult), r=["convS", "vec"], w=[rn("T10")])
                    dve(lambda e: e.scalar_tensor_tensor(Y1, convS[:, j, 1, :], cw(1), Y1, ALU.mult, ALU.add), r=["convS", "vec", rn("T10")], w=[rn("T10")])
                    dve(lambda e: e.scalar_tensor_tensor(Y1, CUE[:, 0:n], cw(2), Y1, ALU.mult, ALU.add), r=[rn("CUE"), "vec", rn("T10")], w=[rn("T10")])
                    pool(lambda e: e.tensor_copy(convout[:, j, 0, 1:17], convS[:, j, 1, :]), r=["convS"], w=["convout"])
                    pool(lambda e: e.tensor_copy(convout[:, j, 1, 1:17], CUE[:, 0:n]), r=[rn("CUE")], w=["convout"])
                proj(1, W["cb"][0], W["cb"][1], tiles0, t0, n)
                dve(lambda e: e.tensor_tensor(Y1, banks[1][:, 0:n], Y1, ALU.mult), r=["B1", rn("T10")], w=[rn("T10"), "B1"])
                proj(0, W["gb"][0], W["gb"][1], tiles0, t0, n)
                sigmoid(UU, banks[0][:, 0:n], ["B0"], [rn("T6"), "B0"])
                pool(lambda e: e.tensor_tensor(Y1, Y1, UU, ALU.mult), r=[rn("T10"), rn("T6")], w=[rn("T10")])
                dve(lambda e: e.tensor_tensor(MG[:, 0:n], YC, Y1, ALU.add), r=[rn("T7"), rn("T10")], w=[rn("MG")])
                for oc in range(KC):
                    bk = oc % 2
                    mm(bk, banks[bk][:, 0:n], W["o"][0][:, oc * 128:(oc + 1) * 128], MG[:, 0:n], True, True, [W["o"][1], rn("MG")])
                    o_ = xT[:, oc, t0:t0 + n]
                    dve(lambda e: e.tensor_tensor(o_, o_, banks[bk][:, 0:n], ALU.add), r=["B%d" % bk], w=["x%d" % oc, "B%d" % bk])

            def wkv_chunks(j, n, sx, b, Y_, rn, H32):
                NCH = n // 128
                KT, BT, KTP, BTP, KKT, RT, VBF, RK, BP, WC = (b[k_] for k_ in ("KT", "BT", "KTP", "BTP", "KKT", "RT", "VBF", "RK", "BP", "WC"))
                pTb = banks[3][:].bitcast(BF16)
                bankM = (4, 6)
                bankT = (5, 3)
                idb2 = ident[:, :].unsqueeze(1).broadcast_to([128, 2, 128])
                h3 = lambda a: a.rearrange("p (h t) -> p h t", t=128)
                st = []
                for c in range(NCH):
                    cs = slice(c * 128, (c + 1) * 128)
                    pb = c % 2
                    tok = "TOK%d" % pb
                    for k_, (src, sres) in enumerate(((VBF, rn("VBF")), (KTP, rn("KTP")), (BTP, rn("BTP")))):
                        S.op("pe", lambda e: e.transpose(pTb[:, k_ * 128:(k_ + 1) * 128], src[:, cs], ident[:]),
                             reads=[sres, "ident"], writes=["B3"])
                    tokv = TOK[:, pb, 0:3, :, :].rearrange("p k s (h c) -> p k (s h) c", c=64)
                    dve(lambda e: e.tensor_copy(tokv[:, :, 0:4:3, :], pTb[:, 0:384].rearrange("p (k h c) -> p k h c", h=2, c=64)),
                        r=["B3"], w=[tok, "B3"])
                for c in range(NCH):
                    cs = slice(c * 128, (c + 1) * 128)
                    pb = c % 2
                    bM, bT = bankM[pb], bankT[pb]
                    rk = RK[:, c, :, :].rearrange("p s t -> p (s t)")
                    mm(bM, banks[bM][:, :], BT[:, cs], rk, True, True, [rn("BT"), rn("RK")])
                    dve(lambda e: e.tensor_tensor(S1[:, pb, :], banks[bM][:, :], mask4[:], ALU.mult), r=["B%d" % bM, "mask4"], w=["S1_%d" % pb, "B%d" % bM])
                    mm(bT, banks[bT][:, :], KT[:, cs], rk, True, True, [rn("KT"), rn("RK")])
                    act(lambda e: e.activation(S2[:, pb, :], banks[bT][:, :], AF.Copy), r=["B%d" % bT], w=["S2_%d" % pb, "B%d" % bT])
                    dve(lambda e: e.tensor_tensor(S2[:, pb, :], S2[:, pb, :], mask4[:], ALU.mult), r=["S2_%d" % pb, "mask4"], w=["S2_%d" % pb])
                    mm(7, banks[7][:, 0:256], KKT[:, cs], BP[:, c, :, :].rearrange("p s t -> p (s t)"), True, True, [rn("KKT"), rn("BP")])
                    dve(lambda e: e.tensor_tensor(h3(LMb[:, pb, :]), h3(banks[7][:, 0:256]), maskL[:, :].unsqueeze(1).broadcast_to([128, 2, 128]), ALU.mult),
                        r=["B7", "maskL"], w=["LM%d" % pb, "B7"])
                    dve(lambda e: e.tensor_tensor(h3(TTb[:, pb, 0, :]), idb2, h3(S1[:, pb, 0:256]), ALU.subtract),
                        r=["ident", "S1_%d" % pb], w=["TT%d_0" % pb])
                    st.append({"M": (LMb[:, pb, :], "LM%d" % pb), "MT": (S1[:, pb, 0:256], "S1_%d" % pb), "t": 0})
                for lev in range(6):
                    mb = lev % 2
                    for c in range(NCH):
                        pb = c % 2
                        bM = bankM[pb]
                        Mcur, MTcur = st[c]["M"], st[c]["MT"]
                        for h in range(2):
                            hs = slice(h * 128, (h + 1) * 128)
                            mm(bM, banks[bM][:, h * 128:(h + 1) * 128], MTcur[0][:, hs], Mcur[0][:, hs], True, True, [Mcur[1], MTcur[1]])
                            mm(bM, banks[bM][:, 256 + h * 128:256 + (h + 1) * 128], Mcur[0][:, hs], MTcur[0][:, hs], True, True, [Mcur[1], MTcur[1]])
                        mres = "MM%d_%d" % (pb, mb)
                        act(lambda e: e.activation(MMb[:, pb, mb, :], banks[bM][:, :], AF.Copy), r=["B%d" % bM], w=[mres, "B%d" % bM])
                        st[c]["M"] = (MMb[:, pb, mb, 0:256], mres)
                        st[c]["MT"] = (MMb[:, pb, mb, 256:512], mres)
                    for c in range(NCH):
                        pb = c % 2
                        bT = bankT[pb]
                        Mcur = st[c]["M"]
                        tcur = st[c]["t"]
                        for h in range(2):
                            hs = slice(h * 128, (h + 1) * 128)
                            mm(bT, banks[bT][:, hs], Mcur[0][:, hs], TTb[:, pb, tcur, hs], True, True, [Mcur[1], "TT%d_%d" % (pb, tcur)])
                        tn = 1 - tcur
                        last = (lev == 5)
                        dst = TTF[:, pb, :] if last else TTb[:, pb, tn, :]
                        dres = ("TTF%d" % pb) if last else ("TT%d_%d" % (pb, tn))
                        dve(lambda e: e.tensor_tensor(dst, banks[bT][:, 0:256], TTb[:, pb, tcur, :], ALU.add),
                            r=["B%d" % bT, "TT%d_%d" % (pb, tcur)], w=[dres, "B%d" % bT])
                        st[c]["t"] = tn
                for c in range(NCH):
                    cs = slice(c * 128, (c + 1) * 128)
                    pb = c % 2
                    tok = "TOK%d" % pb
                    VA, VB_ = TOK[:, pb, 0, 0, :], TOK[:, pb, 0, 1, :]
                    KA, KB = TOK[:, pb, 1, 0, :], TOK[:, pb, 1, 1, :]
                    BA, BB = TOK[:, pb, 2, 0, :], TOK[:, pb, 2, 1, :]
                    UA, UB = TOK[:, pb, 3, 0, :], TOK[:, pb, 3, 1, :]
                    s1, s2 = "S1_%d" % pb, "S2_%d" % pb
                    mm(7, banks[7][:, 0:128], KKT[:, cs], HBD[:, :], True, False, [rn("KKT"), "HBD"])
                    mm(7, banks[7][:, 0:64], S2[:, pb, 0:128], VA[:, 0:64], False, False, [s2, tok])
                    mm(7, banks[7][:, 64:128], S2[:, pb, 128:256], VB_[:, 64:128], False, True, [s2, tok])
                    act(lambda e: e.activation(RHSb[:, :], banks[7][:, 0:128], AF.Copy), r=["B7"], w=["RHSb", "B7"])
                    mm(7, banks[7][:, 128:192], TTF[:, pb, 0:128], RHSb[:, 0:64], True, True, ["TTF%d" % pb, "RHSb"])
                    mm(7, banks[7][:, 192:256], TTF[:, pb, 128:256], RHSb[:, 64:128], True, True, ["TTF%d" % pb, "RHSb"])
                    uv = TOK[:, pb, 3, :, :].rearrange("p s (h c) -> p (s h) c", c=64)
                    act(lambda e: e.activation(uv[:, 0:4:3, :], banks[7][:, 128:256].rearrange("p (h c) -> p h c", c=64), AF.Copy, scale=-1.0),
                        r=["B7"], w=[tok, "B7"])
                    mm(5, banks[5][:, 0:128], HBD[:, :], RT[:, cs], True, False, ["HBD", rn("RT")])
                    mm(5, banks[5][:, 0:128], VA, S2[:, pb, 256:384], False, False, [tok, s2])
                    mm(5, banks[5][:, 0:128], VB_, S2[:, pb, 384:512], False, False, [tok, s2])
                    mm(5, banks[5][:, 0:128], UA, S1[:, pb, 256:384], False, False, [tok, s1])
                    mm(5, banks[5][:, 0:128], UB, S1[:, pb, 384:512], False, True, [tok, s1])
                    mm(3, banks[3][:, 0:128], KA, VA, True, False, [tok])
                    mm(3, banks[3][:, 0:128], KB, VB_, False, False, [tok])
                    mm(3, banks[3][:, 0:128], BA, UA, False, False, [tok])
                    mm(3, banks[3][:, 0:128], BB, UB, False, True, [tok])
                    act(lambda e: e.activation(Y_[:, cs], banks[5][:, 0:128], AF.Copy), r=["B5"], w=[rn("T8"), "B5"])
                    dve(lambda e: e.scalar_tensor_tensor(H32, H32, WC[:, c:c + 1], banks[3][:, 0:128], ALU.mult, ALU.add),
                        r=["B3", rn("WC")], w=["Hst", "B3"])
                    act(lambda e: e.activation(HBD[:, :], H32, AF.Copy), r=["Hst"], w=["HBD"])

            def wkv_sample(j, n, R, V_, SG, KKN, KM, B_, Y_, rn):
                sj3 = SJ.rearrange("p (n v) -> p n v", v=64)
                pp3 = PP.rearrange("p (n v) -> p n v", v=64)
                bc = lambda a: a.unsqueeze(2).broadcast_to([128, NS, 64])
                idb = identH[:, :].unsqueeze(1).broadcast_to([128, NS, 64])
                DEC = SG
                dma_in(SJ, wkvS_d[j, :, :], "SJ", "D_wkvS")
                act(lambda e: e.activation(DEC, SG, AF.Exp, scale=-LAM), r=[rn("T4")], w=[rn("T4")])

                def bsum():
                    for hb, bk in ((0, 6), (1, 7)):
                        mm(bk, banks[bk][:, :], bd1[:], PP[:, hb * 512:(hb + 1) * 512], True, True, ["bd1", "PP"])

                def ps3(hb):
                    bk = 6 if hb == 0 else 7
                    return banks[bk][:, :].rearrange("p (n v) -> p n v", v=64), "B%d" % bk
                dve(lambda e: e.tensor_tensor(pp3, sj3, bc(KKN), ALU.mult), r=["SJ", rn("T10")], w=["PP"])
                bsum()
                dve(lambda e: e.tensor_tensor(sj3, sj3, bc(DEC), ALU.mult), r=[rn("T4")], w=["SJ"])
                for hb in range(2):
                    p3, pr = ps3(hb)
                    ns = slice(hb * 8, hb * 8 + 8)
                    dve(lambda e: e.tensor_tensor(pp3[:, ns, :], p3, bc(B_)[:, ns, :], ALU.mult), r=[pr, rn("T3")], w=["PP", pr])
                dve(lambda e: e.tensor_tensor(SJ, SJ, PP, ALU.subtract), r=["PP"], w=["SJ"])
                dve(lambda e: e.tensor_tensor(pp3, bc(V_), idb, ALU.mult), r=[rn("T2"), "identH"], w=["PP"])
                bsum()
                for hb in range(2):
                    p3, pr = ps3(hb)
                    ns = slice(hb * 8, hb * 8 + 8)
                    dve(lambda e: e.tensor_tensor(pp3[:, ns, :], p3, bc(KM)[:, ns, :], ALU.mult), r=[pr, rn("T11")], w=["PP", pr])
                dve(lambda e: e.tensor_tensor(SJ, SJ, PP, ALU.add), r=["PP"], w=["SJ"])
                dma_out(wkvSo_d[j, :, :], SJ, "SJ", "D_o_wkvS%d" % j)
                dve(lambda e: e.tensor_tensor(pp3, sj3, bc(R), ALU.mult), r=["SJ", rn("T0")], w=["PP"])
                bsum()
                for hb in range(2):
                    p3, pr = ps3(hb)
                    ns = slice(hb * 8, hb * 8 + 8)
                    dve(lambda e: e.tensor_tensor(pp3[:, ns, :], p3, idb[:, ns, :], ALU.mult), r=[pr, "identH"], w=["PP", pr])
                dve(lambda e: e.tensor_reduce(Y_, pp3, AX.X, ALU.add), r=["PP"], w=[rn("T8")])

            def interleave(a, b_):
                out = []
                i = k = 0
                while i < len(a) or k < len(b_):
                    if k >= len(b_) or (i < len(a) and i * len(b_) <= k * len(a)):
                        out.append(a[i]); i += 1
                    else:
                        out.append(b_[k]); k += 1
                return out

            MT = [(t * NM, NM) for t in range(NP_ // NM)] + [(NP_, NS)]
            NTILES = [TILES[0:2], TILES[2:5]]
            unit_no = 0
            for half, tiles in enumerate([MT[0:4], MT[4:9]]):
                tiles0 = tiles[0][0]
                rmsnorm(V_NMIX, NTILES[half], hTh, "hTh", sq_ap, nscr[:, 0, :], rsres="nscr0")
                for q, off in ((24, OFF_L1), (25, OFF_L2)):
                    wt, wres = wload(w_in[:, off:off + 128], "k8", slot=0)
                    for (t0, n) in NTILES[half]:
                        lt0 = t0 - tiles0
                        dst = nscr[:, 0, 0:n]
                        proj(0, wt, wres, tiles0, t0, n)
                        shift_evac(0, q, n, n == NS, dst, "nscr0", nscr[:, 1, :], nscr[:, 2, :], "nscr1", "nscr2")
                        if q == 24:
                            act(lambda e: e.activation(lin1[0:64, lt0:lt0 + n], dst[0:64, :], AF.Tanh), r=["nscr0"], w=["lin1"])
                            act(lambda e: e.activation(lin1[64:128, lt0:lt0 + n], dst[64:128, :], AF.Copy), r=["nscr0"], w=["lin1"])
                        else:
                            act(lambda e: e.activation(dst, dst, AF.Tanh, scale=0.5), r=["nscr0"], w=["nscr0"])
                            dve(lambda e: e.tensor_scalar(lin2[:, lt0:lt0 + n], dst, 0.5, 0.5, ALU.mult, ALU.add), r=["nscr0"], w=["lin2"])
                pend = []
                for j in range(KC):
                    W = {}
                    S.capture = []
                    for si, (nm, off) in enumerate((("r", OFF_R), ("k", OFF_K), ("v", OFF_V))):
                        W[nm] = wload(w_in[:, off + j * 128:off + (j + 1) * 128], "k8", slot=si)
                    wlA = S.capture
                    S.capture = []
                    for si, (nm, off) in enumerate((("ga", OFF_GA), ("cu", OFF_CU), ("cc", OFF_CC), ("cb", OFF_CB), ("gb", OFF_GB))):
                        W[nm] = wload(w_in[:, off + j * 128:off + (j + 1) * 128], "k8", slot=3 + si)
                    W["o"] = wload(w_out[j * 128:(j + 1) * 128, :], "row", slot=8)
                    wlB = S.capture
                    S.capture = None
                    for ti, (t0, n) in enumerate(tiles):
                        sx = unit_no % 2
                        unit_no += 1
                        S.capture = []
                        phase_A(j, t0, n, tiles0, sx, W, ti == 0)
                        la = (wlA if ti == 0 else []) + S.capture
                        S.capture = []
                        phase_B(j, t0, n, tiles0, sx, W, ti == 0, half == 1 and ti == len(tiles) - 1)
                        lb = S.capture
                        S.capture = None
                        sp_ = lb.index("SPLIT")
                        lb1, lb2 = lb[:sp_], (wlB if ti == 0 else []) + lb[sp_ + 1:]
                        pend.append([la, lb1, lb2])
                K_ = len(pend)
                win = []
                for k in range(K_ + 2):
                    l2 = (pend[k - 2][2] if 0 <= k - 2 < K_ else []) + (pend[k][0] if k < K_ else [])
                    l1 = pend[k - 1][1] if 0 <= k - 1 < K_ else []
                    if LIST_SCHED:
                        win += l1 + l2
                        if (k % WIN_STEPS) == WIN_STEPS - 1 or k == K_ + 1:
                            S.sched_flush(win)
                            win = []
                    elif INTERLEAVE:
                        S.merge_flush(l2, l1)
                    else:
                        S.flush(l1 + l2)
            pool(lambda e: e.tensor_copy(shout[:, :, 0], carr[:, :]), r=["carr%d" % q for q in range(26)], w=["shout"])
            dma_out(shout_d[:, :], shout[:].rearrange("p a b -> p (a b)"), "shout", "D_o_sh")
            dma_out(convout_d[:, :], convout[:].rearrange("p a b c -> p (a b c)"), "convout", "D_o_cv")

        def ple():
            S.barrier()
            hT = carve16(0, KC * NTOK).rearrange("p (k t) -> p k t", t=NTOK)
            pTb = carve16(8256, 2 * NTOK).rearrange("p (k t) -> p k t", t=NTOK)
            sq_ap = carve16(10320, 1024).rearrange("p (a t) -> p a t", t=512)
            rs_ap = carve32(10832, 512)
            gts = [carve32(11344, 512), carve32(15984, 512)]
            it_ = 0
            pst = carve32(11856, 2 * NTOK).rearrange("p (k t) -> p k t", t=NTOK)
            rmsnorm(V_NPLE, FT, hT, "hT", sq_ap, rs_ap)
            for k in range(2):
                dma_in(pst[:, k, :], pT_d[k * 128:(k + 1) * 128, :], "pst%d" % k, "D_pst%d" % k)
                pool(lambda e, k=k: e.tensor_copy(pTb[:, k, :], pst[:, k, :]), r=["pst%d" % k], w=["pTb"])
            for oc in range(KC):
                wg, wgr = wload(w_pg[:, oc * 128:(oc + 1) * 128], "k8")
                wp, wpr = wload(w_pp[:, oc * 128:(oc + 1) * 128], "k2")
                for (t0, n) in FT:
                    b0 = (it_ % 2) * 2
                    b1 = b0 + 1
                    gt = gts[it_ % 2]
                    gres = "gt%d" % (it_ % 2)
                    it_ += 1
                    for kc in range(KC):
                        mm(b0, banks[b0][:, 0:n], wg[:, kc, :], hT[:, kc, t0:t0 + n], kc == 0, kc == KC - 1, [wgr, "hT"])
                    for k in range(2):
                        mm(b1, banks[b1][:, 0:n], wp[:, k, :], pTb[:, k, t0:t0 + n], k == 0, k == 1, [wpr, "pTb"])
                    sigmoid(gt[:, 0:n], banks[b0][:, 0:n], ["B%d" % b0], [gres, "B%d" % b0])
                    dve(lambda e: e.tensor_tensor(gt[:, 0:n], gt[:, 0:n], banks[b1][:, 0:n], ALU.mult), r=[gres, "B%d" % b1], w=[gres, "B%d" % b1])
                    o_ = xT[:, oc, t0:t0 + n]
                    dve(lambda e: e.tensor_tensor(o_, o_, gt[:, 0:n], ALU.add), r=[gres], w=["x%d" % oc])

        def final_norm():
            S.barrier()
            sq_ap = carve16(0, 1024).rearrange("p (a t) -> p a t", t=512)
            rs_ap = carve32(512, 512)
            rmsnorm(V_NFIN, FT, None, None, sq_ap, rs_ap, inplace=True)
            for kc in range(KC):
                dma_out(yT_d[kc * 128:(kc + 1) * 128, :], xT[:, kc, :], "x%d" % kc, "D_o_y%d" % kc)

        ffn(w_f1g, w_f1u, w_f1d, V_NF1)
        mixer()
        ffn(w_f2g, w_f2u, w_f2d, V_NF2)
        ple()
        final_norm()
        S.emit(nc, stack)
    return nc


_NC_CACHE = {}


def _fm(v):
    v = np.asarray(v, np.float32).reshape(-1, 128)
    return np.ascontiguousarray(v.T)


def kernel(x_prompt, x_sample, p_prompt, p_sample, state_wkv, state_shift, state_conv,
           norm_ffn1, w_ffn1_gate, w_ffn1_up, w_ffn1_down,
           norm_mix, w_in, mu_shift, w_decay_up, decay_bias, w_iclr_up, iclr_bias,
           w_gate_up, k_k, k_a, r_k, gn_w, gn_b, conv_w, w_out,
           norm_ffn2, w_ffn2_gate, w_ffn2_up, w_ffn2_down,
           norm_ple, w_ple_gate, w_ple_proj, norm_final):
    f = lambda a: np.ascontiguousarray(np.asarray(a, np.float32))
    x_prompt, x_sample, p_prompt, p_sample = f(x_prompt), f(x_sample), f(p_prompt), f(p_sample)
    state_wkv, state_shift, state_conv = f(state_wkv), f(state_shift), f(state_conv)
    vecs = np.concatenate([
        _fm(norm_ffn1[0]), _fm(norm_mix[0]), _fm(norm_ffn2[0]), _fm(norm_ple[0]), _fm(norm_final),
        _fm(decay_bias[0]), _fm(iclr_bias[0]), _fm(k_k[0]), _fm(k_a[0]), _fm(np.asarray(r_k[0]).reshape(-1)),
        _fm(gn_w[0]), _fm(gn_b[0]), _fm(conv_w[0, 0]), _fm(conv_w[0, 1]), _fm(conv_w[0, 2]), _fm(mu_shift[0])], axis=1)
    assert vecs.shape == (128, NV)
    shared = {
        "vecs": f(vecs),
        "w_f1g": f(w_ffn1_gate[0]), "w_f1u": f(w_ffn1_up[0]), "w_f1d": f(w_ffn1_down[0]),
        "w_f2g": f(w_ffn2_gate[0]), "w_f2u": f(w_ffn2_up[0]), "w_f2d": f(w_ffn2_down[0]),
        "w_in": f(w_in[0]), "w_du": f(w_decay_up[0]), "w_iu": f(w_iclr_up[0]), "w_gu": f(w_gate_up[0]),
        "w_out": f(w_out[0]), "w_pg": f(w_ple_gate[0]), "w_pp": f(w_ple_proj[0]),
    }
    in_maps = []
    for i in range(NCORE):
        ss = slice(i * NS, (i + 1) * NS)
        xT = np.concatenate([x_prompt[i].T, x_sample[ss, 0, :].T], axis=1)
        pT = np.concatenate([p_prompt[0, i].T, p_sample[0, ss, 0, :].T], axis=1)
        shT = state_shift[0, ss, :].T.reshape(26, 128, NS).transpose(1, 0, 2).reshape(128, 26 * NS)
        cvT = state_conv[0, ss].transpose(2, 1, 0).reshape(KC, 128, 2, NS).transpose(1, 0, 2, 3).reshape(128, KC * 2 * NS)
        sw = state_wkv[0, ss].reshape(NS, KC, 2, 64, 64).transpose(1, 2, 4, 0, 3).reshape(KC, 128, NS * 64)
        m = dict(shared)
        m.update({"xT": f(xT), "pT": f(pT), "shiftS": f(shT), "convS": f(cvT), "wkvS": f(sw)})
        in_maps.append(m)
    if "nc" not in _NC_CACHE:
        _NC_CACHE["nc"] = build_program()
    res = run_bass_kernel_spmd(_NC_CACHE["nc"], in_maps, core_ids=list(range(NCORE)))
    R = res.results
    if DEBUG:
        DBG_RESULTS.clear()
        DBG_RESULTS.update({k: np.asarray(R[0][k]).astype(np.float32) for k in DBG_NAMES})
    y_prompt = np.stack([R[i]["yT"][:, :NP_].T for i in range(NCORE)], 0)
    y_sample = np.concatenate([R[i]["yT"][:, NP_:].T for i in range(NCORE)], 0)[:, None, :]
    wkv_p = np.stack([R[i]["wkvP"].reshape(KC, 2, 64, 64).transpose(0, 1, 3, 2).reshape(16, 64, 64) for i in range(NCORE)], 0)[None]
    sh_all = [R[i]["shout"].reshape(128, 26, 17).transpose(1, 0, 2).reshape(SHIFT_COLS, 17) for i in range(NCORE)]
    sh_p = np.stack([s[:, 0] for s in sh_all], 0)[None]
    sh_s = np.concatenate([s[:, 1:].T for s in sh_all], 0)[None]
    cv_all = [R[i]["convout"].reshape(128, KC, 2, 17).transpose(1, 0, 2, 3).reshape(D, 2, 17) for i in range(NCORE)]
    cv_p = np.stack([c[:, :, 0].T for c in cv_all], 0)[None]
    cv_s = np.concatenate([c[:, :, 1:].transpose(2, 1, 0) for c in cv_all], 0)[None]
    wkv_s = np.concatenate([R[i]["wkvSo"].reshape(KC, 2, 64, NS, 64).transpose(3, 0, 1, 4, 2).reshape(NS, 16, 64, 64)
                            for i in range(NCORE)], 0)[None]
    c = lambda a: np.ascontiguousarray(a, dtype=np.float32)
    return (c(y_prompt), c(y_sample), c(wkv_p), c(sh_p), c(cv_p), c(wkv_s), c(sh_s), c(cv_s))
```

```python
import contextlib
import numpy as np
import concourse.bass as bass
import concourse.mybir as mybir
from concourse.bass_utils import run_bass_kernel_spmd

F32 = mybir.dt.float32
BF16 = mybir.dt.bfloat16
AF = mybir.ActivationFunctionType
ALU = mybir.AluOpType
AX = mybir.AxisListType

D = 1024
KC = 8
DFF = 2816
NF = 22
NP_ = 2048
NS = 16
NTOK = NP_ + NS
NCORE = 8
LAM = float(np.exp(-0.5))
RMS_EPS = 1e-6
GN_EPS = 64e-5
SHIFT_COLS = 3328
IN_COLS = 8448
OFF_R, OFF_K, OFF_V, OFF_L1, OFF_L2 = 0, 1024, 2048, 3072, 3200
OFF_CB, OFF_CC, OFF_CU, OFF_GA, OFF_GB = 3328, 4352, 5376, 6400, 7424
V_NF1, V_NMIX, V_NF2, V_NPLE, V_NFIN, V_DB, V_IB, V_KK, V_KA, V_RK, V_GNW, V_GNB, V_CW, V_MU = (
    0, 8, 16, 24, 32, 40, 48, 56, 64, 72, 80, 88, 96, 120)
NV = 146
NSTG = 2
NW = 9


class _Rec:
    def __init__(self):
        self.call = None

    def __getattr__(self, name):
        def f(*a, **k):
            self.call = (name, a, k)
            return self
        return f


class Sched:
    ENG = ("pe", "act", "dve", "pool", "sp")

    def __init__(self):
        self.ops = {e: [] for e in self.ENG}
        self.count = {e: 0 for e in self.ENG}
        self.waited = {e: {} for e in self.ENG}
        self.last_write = {}
        self.readers = {}
        self.dmacount = {}
        self.out_events = []
        self.pending_barrier = {e: {} for e in self.ENG}
        self.capture = None
        self.opsize = {e: {} for e in self.ENG}
        self.sim_eng = {}
        self.sim_w = {}
        self.sim_r = {}

    def barrier(self):
        snap = {"S_" + e: self.count[e] for e in self.ENG if self.count[e] > 0}
        snap.update(self.dmacount)
        for e in self.ENG:
            for s, v in snap.items():
                if self.pending_barrier[e].get(s, 0) < v:
                    self.pending_barrier[e][s] = v

    def op(self, eng, fn, reads=(), writes=(), dma=None, final=False):
        rec = _Rec()
        fn(rec)
        item = (eng, rec.call, tuple(reads), tuple(writes), dma, final)
        if self.capture is not None:
            self.capture.append(item)
            return None
        return self._op(*item)

    def flush(self, items):
        for it in items:
            self._op(*it)

    def _sim_cost(self, item):
        eng, call, reads, writes, dma, final = item
        cname, cargs, ckw = call
        out = ckw.get("out", cargs[0] if cargs else None)
        try:
            n = 1
            for d in out.shape[1:]:
                n *= int(d)
        except Exception:
            n = 64
        if eng == "pe":
            dur, lat = 0.03 + n * 0.00083, 0.25
        elif eng == "sp":
            dur, lat = 0.05, 3.0
        elif eng == "pool":
            dur, lat = 0.25 + n * 0.0022, 0.2
        else:
            f = 2.0 if cname in ("tensor_tensor", "scalar_tensor_tensor", "tensor_tensor_scan") and not any(w.startswith("B") and w[1:].isdigit() for w in writes) else 1.0
            dur = (0.22 if eng == "act" else 0.08) + n * 0.00105 * f
            if n <= 4:
                dur = 0.3
            lat = 0.15
        t = self.sim_eng.get(eng, 0.0)
        for r in reads:
            tw = self.sim_w.get(r)
            if tw is not None:
                t = max(t, tw[0] + (0.0 if tw[1] == eng else 0.1))
        for w in writes:
            tw = self.sim_w.get(w)
            if tw is not None:
                t = max(t, tw[0] + (0.0 if tw[1] == eng else 0.1))
            tr = self.sim_r.get(w)
            if tr is not None:
                t = max(t, tr)
        return t, dur, lat

    def _sim_commit(self, item):
        eng, call, reads, writes, dma, final = item
        t, dur, lat = self._sim_cost(item)
        self.sim_eng[eng] = t + dur
        end = t + dur + lat
        for w in writes:
            self.sim_w[w] = (end, eng)
            self.sim_r[w] = 0.0
        for r in reads:
            if r not in writes and self.sim_r.get(r, 0.0) < end:
                self.sim_r[r] = end

    def sched_flush(self, items):
        n = len(items)
        preds = [set() for _ in range(n)]
        lastw, readers = {}, {}
        for i, it in enumerate(items):
            eng, call, reads, writes, dma, final = it
            for r in reads:
                if r in lastw:
                    preds[i].add(lastw[r])
            for w in writes:
                if w in lastw:
                    preds[i].add(lastw[w])
                for rr in readers.get(w, ()):
                    preds[i].add(rr)
            for w in writes:
                lastw[w] = i
                readers[w] = []
            for r in reads:
                if r not in writes:
                    readers.setdefault(r, []).append(i)
            if eng == "sp":
                if "sp" in lastw.get("__q", {}):
                    preds[i].add(lastw["__q"]["sp"])
                lastw.setdefault("__q", {})["sp"] = i
        succs = [[] for _ in range(n)]
        for i in range(n):
            preds[i].discard(i)
            for p in preds[i]:
                succs[p].append(i)
        dur = [self._sim_cost(it)[1] + self._sim_cost(it)[2] for it in items]
        blevel = [0.0] * n
        for i in range(n - 1, -1, -1):
            b = 0.0
            for q in succs[i]:
                if blevel[q] > b:
                    b = blevel[q]
            blevel[i] = b + dur[i]
        indeg = [len(p) for p in preds]
        ready = [i for i in range(n) if indeg[i] == 0]
        while ready:
            st = [(self._sim_cost(items[i])[0], i) for i in ready]
            tmin = min(t for t, _ in st)
            cand = [i for t, i in st if t <= tmin + SCHED_SLACK]
            pick = max(cand, key=lambda i: (blevel[i], -i))
            ready.remove(pick)
            self._op(*items[pick])
            for q in succs[pick]:
                indeg[q] -= 1
                if indeg[q] == 0:
                    ready.append(q)

    def merge_flush(self, la, lb):
        i = k = 0
        while i < len(la) or k < len(lb):
            if i >= len(la):
                pick_a = False
            elif k >= len(lb):
                pick_a = True
            else:
                pick_a = self._sim_cost(la[i])[0] + MERGE_BIAS < self._sim_cost(lb[k])[0]
            if pick_a:
                it = la[i]; i += 1
            else:
                it = lb[k]; k += 1
            self._op(*it)

    def _op(self, eng, call, reads, writes, dma, final):
        self._sim_commit((eng, call, reads, writes, dma, final))
        cname, cargs, ckw = call
        out_ = ckw.get("out", cargs[0] if cargs else None)
        try:
            nfree = 1
            for d_ in out_.shape[1:]:
                nfree *= int(d_)
        except Exception:
            nfree = 0
        fn = lambda e, cname=cname, cargs=cargs, ckw=ckw: getattr(e, cname)(*cargs, **ckw)
        waits = {}
        mysem = "S_" + eng

        def need(ev, war=False):
            if ev is None:
                return
            s, v = ev
            if s == mysem:
                if eng in ("pe", "sp"):
                    return
                if not STRICT_SYNC:
                    if war or v <= self.count[eng] - 3:
                        return
                    if min(nfree, self.opsize[eng].get(v, 0)) > SHORT_OP:
                        return
            if self.waited[eng].get(s, 0) >= v:
                return
            if waits.get(s, 0) < v:
                waits[s] = v

        if self.pending_barrier[eng]:
            for s, v in self.pending_barrier[eng].items():
                need((s, v))
            self.pending_barrier[eng] = {}
        for r in reads:
            need(self.last_write.get(r))
        for w in writes:
            need(self.last_write.get(w))
            for s, v in self.readers.get(w, {}).items():
                need((s, v), war=True)
        if dma is not None:
            self.dmacount[dma] = self.dmacount.get(dma, 0) + 16
            ev = (dma, self.dmacount[dma])
        else:
            self.count[eng] += 1
            ev = (mysem, self.count[eng])
            self.opsize[eng][self.count[eng]] = nfree
        for w in writes:
            self.last_write[w] = ev
            self.readers[w] = {}
        for r in reads:
            if r in writes:
                continue
            d = self.readers.setdefault(r, {})
            if d.get(ev[0], 0) < ev[1]:
                d[ev[0]] = ev[1]
        for s, v in waits.items():
            self.waited[eng][s] = v
        self.ops[eng].append((list(waits.items()), fn, ev, dma is not None))
        if final:
            self.out_events.append(ev)
        return ev

    def emit(self, nc, stack):
        names = ["S_" + e for e in self.ENG] + sorted(self.dmacount.keys())
        sems = {n: stack.enter_context(nc.semaphore(n)) for n in names}
        block = stack.enter_context(nc.Block())
        fin = {}
        for s, v in self.out_events:
            fin[s] = max(fin.get(s, 0), v)

        def replay(engname):
            def body(e):
                for waits, fn, ev, isdma in self.ops[engname]:
                    for s, v in waits:
                        e.wait_ge(sems[s], v)
                    fn(e).then_inc(sems[ev[0]], 16 if isdma else 1)
                if engname == "sp":
                    for s, v in fin.items():
                        e.wait_ge(sems[s], v)
            return body

        block.tensor(replay("pe"))
        block.scalar(replay("act"))
        block.vector(replay("dve"))
        block.gpsimd(replay("pool"))
        block.sync(replay("sp"))


DEBUG = False
INTERLEAVE = True
SCHED_SLACK = 0.3
LIST_SCHED = True
STRICT_SYNC = False
WIN_STEPS = 2
SHORT_OP = 64
MERGE_BIAS = -1.0
DBG_NAMES = []
DBG_RESULTS = {}


def build_program():
    nc = bass.Bass("TRN2", target_bir_lowering=False)
    S = Sched()
    stack = contextlib.ExitStack()
    del DBG_NAMES[:]

    def dbg(name, ap, res, shape, dt=F32):
        if not DEBUG:
            return
        d = nc.dram_tensor("dbg_" + name, list(shape), dt, kind="ExternalOutput").ap()
        DBG_NAMES.append("dbg_" + name)
        idx = tuple(slice(None) for _ in shape)
        S.op("sp", lambda e: e.dma_start(out=d[idx], in_=ap), reads=list(res), dma="D_dbg_" + name, final=True)

    def din(name, shape):
        return nc.dram_tensor(name, list(shape), F32, kind="ExternalInput").ap()

    def dout(name, shape):
        return nc.dram_tensor(name, list(shape), F32, kind="ExternalOutput").ap()

    xT_d = din("xT", [D, NTOK])
    pT_d = din("pT", [256, NTOK])
    vec_d = din("vecs", [128, NV])
    shiftS_d = din("shiftS", [128, 26 * NS])
    convS_d = din("convS", [128, KC * 2 * NS])
    wkvS_d = din("wkvS", [KC, 128, NS * 64])
    w_f1g = din("w_f1g", [D, DFF]); w_f1u = din("w_f1u", [D, DFF]); w_f1d = din("w_f1d", [DFF, D])
    w_f2g = din("w_f2g", [D, DFF]); w_f2u = din("w_f2u", [D, DFF]); w_f2d = din("w_f2d", [DFF, D])
    w_in = din("w_in", [D, IN_COLS])
    w_du = din("w_du", [64, D]); w_iu = din("w_iu", [64, D]); w_gu = din("w_gu", [128, D])
    w_out = din("w_out", [D, D]); w_pg = din("w_pg", [D, D]); w_pp = din("w_pp", [256, D])
    yT_d = dout("yT", [D, NTOK])
    wkvP_d = dout("wkvP", [KC, 128, 64])
    shout_d = dout("shout", [128, 26 * 17])
    convout_d = dout("convout", [128, KC * 2 * 17])
    wkvSo_d = dout("wkvSo", [KC, 128, NS * 64])

    def sb(name, shape, dt=F32):
        return stack.enter_context(nc.sbuf_tensor(name, list(shape), dt))

    with stack:
        xT = sb("xT_sb", [128, KC, NTOK])
        vec = sb("vec", [128, NV])
        ommu = sb("ommu", [128, 26])
        omka = sb("omka", [128, KC])
        hbias = sb("hbias", [128, 2 * KC])
        ident = sb("ident", [128, 128], BF16)
        ones_bf = sb("ones_bf", [128, 128], BF16)
        bd1 = sb("bd1", [128, 128])
        bdm = sb("bdm", [128, 128])
        bdrk = sb("bdrk", [128, 128])
        identH = sb("identH", [128, 64])
        mask4 = sb("mask4", [128, 512], BF16)
        maskL = sb("maskL", [128, 128], BF16)
        resetm = sb("resetm", [128, 256], BF16)
        stg = sb("stg", [128, NSTG, 1024])
        wring = sb("wring", [128, NW, 1024], BF16)
        carr = sb("carr", [128, 26])
        shout = sb("shout_sb", [128, 26, 17])
        shiftS = sb("shiftS_sb", [128, 26, NS])
        convS = sb("convS_sb", [128, KC, 2, NS])
        convout = sb("convout_sb", [128, KC, 2, 17])
        ccar = sb("ccar", [128, KC, 2])
        Hst = sb("Hst", [128, KC, 128])
        ARENA = 26328
        arena = sb("arena", [128, ARENA])

        def carve32(off, n):
            return arena[:, off:off + n]

        def carve16(off, n):
            return arena[:, off:off + n // 2].bitcast(BF16)

        banks = [stack.enter_context(nc.psum_tensor("bank%d" % i, [128, 512], F32)) for i in range(8)]

        cnt = {"stg": 0, "w": 0}
        wgen = [0] * NW

        def dve(fn, r=(), w=()):
            S.op("dve", fn, reads=r, writes=w)

        def act(fn, r=(), w=()):
            S.op("act", fn, reads=r, writes=w)

        def pool(fn, r=(), w=()):
            S.op("pool", fn, reads=r, writes=w)

        def mm(bank, out_ap, lhsT, rhs, start, stop, r):
            S.op("pe", lambda e: e.matmul(out_ap, lhsT, rhs, start=start, stop=stop, skip_group_check=True),
                 reads=r, writes=["B%d" % bank])

        def dma_in(out_ap, in_ap, res, key):
            S.op("sp", lambda e: e.dma_start(out=out_ap, in_=in_ap), writes=([res] if isinstance(res, str) else list(res)), dma=key)

        def dma_out(out_ap, in_ap, res, key):
            S.op("sp", lambda e: e.dma_start(out=out_ap, in_=in_ap), reads=([res] if isinstance(res, str) else list(res)), dma=key, final=True)

        def stage_load(src_ap, view_fn, prt=(0, 128)):
            si = cnt["stg"] % NSTG
            cnt["stg"] += 1
            flat = stg[prt[0]:prt[1], si, :]
            S.op("sp", lambda e: e.dma_start(out=view_fn(flat), in_=src_ap), writes=["stg%d" % si], dma="D_stg%d" % si)
            return flat, "stg%d" % si

        def wload(src_ap, kind, slot=None):
            if slot is None:
                wi = cnt["w"] % NW
                cnt["w"] += 1
            else:
                wi = slot
            wgen[wi] += 1
            if kind == "k8":
                vf = lambda a: a.rearrange("p (k c) -> p k c", c=128)
                src = src_ap.rearrange("(k p) c -> p k c", p=128)
            elif kind == "k2":
                vf = lambda a: a[:, 0:256].rearrange("p (k c) -> p k c", c=128)
                src = src_ap.rearrange("(k p) c -> p k c", p=128)
            else:
                vf = lambda a: a
                src = src_ap
            flat, sres = stage_load(src, vf)
            dst = wring[:, wi, :]
            n = 256 if kind == "k2" else 1024
            pool(lambda e: e.tensor_copy(dst[:, 0:n], flat[:, 0:n]), r=[sres], w=["w%d" % wi])
            return vf(dst), "w%d" % wi

        TILES = [(0, 512), (512, 512), (1024, 512), (1536, 512), (2048, 16)]
        FT = [(0, 413), (413, 413), (826, 413), (1239, 413), (1652, 412)]

        dma_in(vec[:], vec_d[:, :], "vec", "D_vec")
        for kc in range(KC):
            dma_in(xT[:, kc, :], xT_d[kc * 128:(kc + 1) * 128, :], "x%d" % kc, "D_x%d" % kc)
        dma_in(shiftS[:].rearrange("p a b -> p (a b)"), shiftS_d[:, :], "shiftS", "D_shiftS")
        dma_in(convS[:].rearrange("p a b c -> p (a b c)"), convS_d[:, :], "convS", "D_convS")
        pool(lambda e: e.memset(ident[:], 0.0), w=["ident"])
        pool(lambda e: e.affine_select(ident[:], ident[:], [[-1, 128]], ALU.not_equal, 1.0, base=0, channel_multiplier=1),
             r=["ident"], w=["ident"])
        pool(lambda e: e.memset(ones_bf[:], 1.0), w=["ones_bf"])
        pool(lambda e: e.memset(bd1[:], 0.0), w=["bd1"])
        pool(lambda e: e.memset(bd1[0:64, 0:64], 1.0), w=["bd1"])
        pool(lambda e: e.memset(bd1[64:128, 64:128], 1.0), w=["bd1"])
        pool(lambda e: e.tensor_scalar(bdm[:], bd1[:], 1.0 / 64.0, None, ALU.mult), r=["bd1"], w=["bdm"])
        pool(lambda e: e.tensor_copy(identH[0:64, :], ident[0:64, 0:64]), r=["ident"], w=["identH"])
        pool(lambda e: e.tensor_copy(identH[64:128, :], ident[64:128, 64:128]), r=["ident"], w=["identH"])
        pool(lambda e: e.memset(mask4[:], 1.0), w=["mask4"])
        for b in range(4):
            base = -1 if b < 2 else 0
            blk = mask4[:, b * 128:(b + 1) * 128]
            pool(lambda e, blk=blk, base=base: e.affine_select(blk, blk, [[1, 128]], ALU.is_ge, 0.0, base=base, channel_multiplier=-1),
                 r=["mask4"], w=["mask4"])
        pool(lambda e: e.memset(maskL[:], 1.0), w=["maskL"])
        pool(lambda e: e.affine_select(maskL[:], maskL[:], [[-1, 128]], ALU.is_ge, 0.0, base=-1, channel_multiplier=1),
             r=["maskL"], w=["maskL"])
        pool(lambda e: e.memset(resetm[:], 1.0), w=["resetm"])
        pool(lambda e: e.memset(resetm[:].rearrange("p (c t) -> p c t", t=128)[:, :, 0:1], 0.0), w=["resetm"])
        pool(lambda e: e.memset(carr[:], 0.0), w=["carr"])
        pool(lambda e: e.memset(ccar[:], 0.0), w=["ccar"])
        pool(lambda e: e.memset(Hst[:], 0.0), w=["Hst"])
        dve(lambda e: e.tensor_scalar(ommu[:], vec[:, V_MU:V_MU + 26], -1.0, 1.0, ALU.mult, ALU.add), r=["vec"], w=["ommu"])
        dve(lambda e: e.tensor_scalar(omka[:], vec[:, V_KA:V_KA + 8], -1.0, 1.0, ALU.mult, ALU.add), r=["vec"], w=["omka"])
        dve(lambda e: e.tensor_scalar(hbias[:, 0:KC], vec[:, V_IB:V_IB + 8], 0.5, None, ALU.mult), r=["vec"], w=["hbias"])
        dve(lambda e: e.tensor_scalar(hbias[:, KC:2 * KC], vec[:, V_DB:V_DB + 8], 0.5, None, ALU.mult), r=["vec"], w=["hbias"])
        consts = ["vec", "ommu", "omka"]

        def sigmoid(out, in_, r, w, bias=None, eng_fix="dve"):
            if bias is None:
                act(lambda e: e.activation(out, in_, AF.Tanh, scale=0.5), r=r, w=w)
            else:
                act(lambda e: e.activation(out, in_, AF.Tanh, scale=0.5, bias=bias), r=r, w=w)
            wo = [x for x in w if not x.startswith("B")]
            dve(lambda e: e.tensor_scalar(out, out, 0.5, 0.5, ALU.mult, ALU.add), r=wo, w=wo)


        def rmsnorm(vcol, tiles, hdst, hres, sq_ap, rs_ap, inplace=False, rsres="rs"):
            for ti, (t0, n) in enumerate(tiles):
                for kc in range(KC):
                    sq = sq_ap[:, kc % 2, 0:n]
                    act(lambda e, sq=sq, kc=kc: e.activation(sq, xT[:, kc, t0:t0 + n], AF.Square),
                        r=["x%d" % kc], w=["sq%d" % (kc % 2)])
                    mm(2, banks[2][:, 0:n], ones_bf[:], sq, kc == 0, kc == KC - 1, ["ones_bf", "sq%d" % (kc % 2)])
                rs = rs_ap[:, 0:n]
                dve(lambda e, rs=rs, n=n: e.tensor_scalar(rs, banks[2][:, 0:n], 1.0 / D, RMS_EPS, ALU.mult, ALU.add),
                    r=["B2"], w=[rsres, "B2"])
                act(lambda e, rs=rs: e.activation(rs, rs, AF.Ln), r=[rsres], w=[rsres])
                act(lambda e, rs=rs: e.activation(rs, rs, AF.Exp, scale=-0.5), r=[rsres], w=[rsres])
                for kc in range(KC):
                    if inplace:
                        o = xT[:, kc, t0:t0 + n]
                        dve(lambda e, o=o, kc=kc, rs=rs: e.scalar_tensor_tensor(o, o, vec[:, vcol + kc:vcol + kc + 1], rs, ALU.mult, ALU.mult),
                            r=[rsres, "vec"], w=["x%d" % kc])
                    else:
                        lt0 = t0 - tiles[0][0]
                        o = hdst[:, kc, lt0:lt0 + n]
                        dve(lambda e, o=o, kc=kc, rs=rs: e.scalar_tensor_tensor(o, xT[:, kc, t0:t0 + n], vec[:, vcol + kc:vcol + kc + 1], rs, ALU.mult, ALU.mult),
                            r=[rsres, "vec", "x%d" % kc], w=[hres])

        def ffn(wg_d, wu_d, wd_d, vcol):
            S.barrier()
            hT = carve16(0, KC * NTOK).rearrange("p (k t) -> p k t", t=NTOK)
            actb = carve16(8256, 4 * NTOK).rearrange("p (f t) -> p f t", t=NTOK)
            sq_ap = carve16(12384, 1024).rearrange("p (a t) -> p a t", t=512)
            rs_ap = carve32(12896, 512)
            sil = carve32(13408, 1024).rearrange("p (a t) -> p a t", t=512)
            rmsnorm(vcol, FT, hT, "hT", sq_ap, rs_ap)
            F1 = (vcol == V_NF1)
            if F1:
                dbg("f_h", hT[:, 0, 0:1024], ["hT"], [128, 1024], BF16)
                dbg("f_rs", rs_ap, ["rs"], [128, 512])
            groups = [[0, 1, 2, 3], [4, 5, 6, 7], [8, 9, 10, 11], [12, 13, 14, 15], [16, 17, 18], [19, 20, 21]]
            it = 0
            for grp in groups:
                for fl, f in enumerate(grp):
                    wg, wgr = wload(wg_d[:, f * 128:(f + 1) * 128], "k8")
                    wu, wur = wload(wu_d[:, f * 128:(f + 1) * 128], "k8")
                    for (t0, n) in FT:
                        bg = (it % 2) * 2
                        bu = bg + 1
                        it += 1
                        for kc in range(KC):
                            mm(bg, banks[bg][:, 0:n], wg[:, kc, :], hT[:, kc, t0:t0 + n], kc == 0, kc == KC - 1, [wgr, "hT"])
                        for kc in range(KC):
                            mm(bu, banks[bu][:, 0:n], wu[:, kc, :], hT[:, kc, t0:t0 + n], kc == 0, kc == KC - 1, [wur, "hT"])
                        sl = sil[:, bg // 2, 0:n]
                        act(lambda e, sl=sl, bg=bg, n=n: e.activation(sl, banks[bg][:, 0:n], AF.Silu),
                            r=["B%d" % bg], w=["sil%d" % (bg // 2), "B%d" % bg])
                        o = actb[:, fl, t0:t0 + n]
                        dve(lambda e, o=o, sl=sl, bu=bu, n=n: e.tensor_tensor(o, banks[bu][:, 0:n], sl, ALU.mult),
                            r=["sil%d" % (bg // 2), "B%d" % bu], w=["act%d" % fl, "B%d" % bu])
                if F1 and grp[0] == 0:
                    dbg("f_act0", actb[:, 0, 0:1024], ["act0"], [128, 1024], BF16)
                    dbg("f_act3", actb[:, 3, 0:1024], ["act3"], [128, 1024], BF16)
                wds = [wload(wd_d[f * 128:(f + 1) * 128, :], "row") for f in grp]
                for oc in range(KC):
                    for (t0, n) in FT:
                        bk = 4 + (it % 2)
                        it += 1
                        for fl, f in enumerate(grp):
                            mm(bk, banks[bk][:, 0:n], wds[fl][0][:, oc * 128:(oc + 1) * 128], actb[:, fl, t0:t0 + n],
                               fl == 0, fl == len(grp) - 1, [wds[fl][1], "act%d" % fl])
                        o = xT[:, oc, t0:t0 + n]
                        dve(lambda e, o=o, bk=bk, n=n: e.scalar_tensor_tensor(o, banks[bk][:, 0:n], 0.5, o, ALU.mult, ALU.add),
                            r=["B%d" % bk], w=["x%d" % oc, "B%d" % bk])

        def mixer():
            S.barrier()
            HW_ = 1040
            NM = 256
            o = 0
            hTh = carve16(o, KC * HW_).rearrange("p (k t) -> p k t", t=HW_); o += KC * HW_ // 2
            lin1 = carve16(o, HW_); o += HW_ // 2
            lin2 = carve16(o, HW_); o += HW_ // 2
            LUd = carve16(o, 1024); o += 512
            LUi = carve16(o, 1024); o += 512
            WGU = carve16(o, 1024); o += 512
            nscr = carve32(o, 1536).rearrange("p (a t) -> p a t", t=512); o += 1536
            sq_ap = carve16(o, 1024).rearrange("p (a t) -> p a t", t=512); o += 512
            SJ = carve32(o, 1024); o += 1024
            PP = carve32(o, 1024); o += 1024
            RHSb = carve16(o, 128); o += 64
            HBD = carve16(o, 128); o += 64
            NT_ = 12
            BUF = []
            for s_ in range(2):
                b = {}
                b["Tp"] = carve32(o, NT_ * NM).rearrange("p (a t) -> p a t", t=NM); o += NT_ * NM
                b["zraw"] = carve32(o, NM); o += NM
                b["ztmp"] = carve32(o, NM); o += NM
                b["CUE"] = carve32(o, NM + 2); o += NM + 2
                for nm in ("KT", "BT", "KTP", "BTP", "KKT", "RT", "VBF", "MG"):
                    b[nm] = carve16(o, NM); o += NM // 2
                b["RK"] = carve16(o, 4 * NM).rearrange("p (c s t) -> p c s t", s=4, t=128); o += 2 * NM
                b["BP"] = carve16(o, 2 * NM).rearrange("p (c s t) -> p c s t", s=2, t=128); o += NM
                b["WC"] = carve32(o, 2); o += 2
                BUF.append(b)
            TOK = carve16(o, 2048).rearrange("p (b k s t) -> p b k s t", k=4, s=2, t=128); o += 1024
            S1 = carve16(o, 1024).rearrange("p (b t) -> p b t", t=512); o += 512
            S2 = carve16(o, 1024).rearrange("p (b t) -> p b t", t=512); o += 512
            LMb = carve16(o, 512).rearrange("p (b t) -> p b t", t=256); o += 256
            MMb = carve16(o, 2048).rearrange("p (c b t) -> p c b t", b=2, t=512); o += 1024
            TTb = carve16(o, 1024).rearrange("p (c b t) -> p c b t", b=2, t=256); o += 512
            TTF = carve16(o, 512).rearrange("p (b t) -> p b t", t=256); o += 256
            assert o <= ARENA, o

            pool(lambda e: e.memset(LUd[:, :], 0.0), w=["LUd"])
            pool(lambda e: e.memset(LUi[:, :], 0.0), w=["LUi"])
            pool(lambda e: e.memset(TOK[:].rearrange("p b k s t -> p (b k s t)"), 0.0), w=["TOK0", "TOK1"])
            for s_ in range(2):
                pool(lambda e: e.memset(BUF[s_]["RK"][:].rearrange("p c s t -> p (c s t)"), 0.0), w=["RK_%d" % s_])
                pool(lambda e: e.memset(BUF[s_]["BP"][:].rearrange("p c s t -> p (c s t)"), 0.0), w=["BP_%d" % s_])
            fl_, sr = stage_load(w_du[:, :], lambda a: a, prt=(0, 64))
            pool(lambda e: e.tensor_copy(LUd[0:64, :], fl_), r=[sr], w=["LUd"])
            fl_, sr = stage_load(w_iu[:, :], lambda a: a, prt=(64, 128))
            pool(lambda e: e.tensor_copy(LUi[64:128, :], fl_), r=[sr], w=["LUi"])
            fl_, sr = stage_load(w_gu[:, :], lambda a: a)
            pool(lambda e: e.tensor_copy(WGU[:, :], fl_), r=[sr], w=["WGU"])

            def proj(bank, wt, wres, tiles0, t0, n):
                lt0 = t0 - tiles0
                for kc in range(KC):
                    mm(bank, banks[bank][:, 0:n], wt[:, kc, :], hTh[:, kc, lt0:lt0 + n], kc == 0, kc == KC - 1, [wres, "hTh"])

            def shift_evac(bank, q, n, is_sample, dst, dres, zraw, ztmp, zr, zt):
                bk = "B%d" % bank
                act(lambda e: e.activation(zraw[:, 0:n], banks[bank][:, 0:n], AF.Copy), r=[bk], w=[zr, bk])
                act(lambda e: e.activation(ztmp[:, 0:n], banks[bank][:, 0:n], AF.Identity, scale=ommu[:, q:q + 1]),
                    r=[bk, "ommu"], w=[zt, bk])
                mu = vec[:, V_MU + q:V_MU + q + 1]
                if not is_sample:
                    dve(lambda e: e.scalar_tensor_tensor(dst[:, 1:n], zraw[:, 0:n - 1], mu, ztmp[:, 1:n], ALU.mult, ALU.add),
                        r=[zr, zt, "vec"], w=[dres])
                    dve(lambda e: e.scalar_tensor_tensor(dst[:, 0:1], carr[:, q:q + 1], mu, ztmp[:, 0:1], ALU.mult, ALU.add),
                        r=["carr%d" % q, zt, "vec"], w=[dres])
                    dve(lambda e: e.tensor_copy(carr[:, q:q + 1], zraw[:, n - 1:n]), r=[zr], w=["carr%d" % q])
                else:
                    dve(lambda e: e.scalar_tensor_tensor(dst[:, 0:n], shiftS[:, q, :], mu, ztmp[:, 0:n], ALU.mult, ALU.add),
                        r=["shiftS", zt, "vec"], w=[dres])
                    pool(lambda e: e.tensor_copy(shout[:, q, 1:17], zraw[:, 0:n]), r=[zr], w=["shout"])

            def phase_A(j, t0, n, tiles0, sx, W, first_tile):
                b = BUF[sx]
                Tp = b["Tp"]
                rn = lambda nm: "%s_%d" % (nm, sx)
                T = lambda i, n_=n: Tp[:, i, 0:n_]
                lt0 = t0 - tiles0
                smp = (n == NS)
                jc = slice(j * 128, (j + 1) * 128)
                if first_tile:
                    dve(lambda e: e.tensor_scalar(bdrk[:], bd1[:], vec[:, V_RK + j:V_RK + j + 1], None, ALU.mult), r=["bd1", "vec"], w=["bdrk"])
                R, K_, V_, A_, SG, G_ = T(0), T(1), T(2), T(3), T(4), T(5)
                zz = (b["zraw"], b["ztmp"], rn("zraw"), rn("ztmp"))
                proj(0, W["r"][0], W["r"][1], tiles0, t0, n)
                shift_evac(0, j, n, smp, R, rn("T0"), *zz)
                proj(1, W["k"][0], W["k"][1], tiles0, t0, n)
                shift_evac(1, 8 + j, n, smp, K_, rn("T1"), *zz)
                proj(0, W["v"][0], W["v"][1], tiles0, t0, n)
                shift_evac(0, 16 + j, n, smp, V_, rn("T2"), *zz)
                mm(1, banks[1][:, 0:n], LUi[:, jc], lin1[:, lt0:lt0 + n], True, True, ["LUi", "lin1"])
                sigmoid(A_, banks[1][:, 0:n], ["B1", "hbias"], [rn("T3"), "B1"], bias=hbias[:, j:j + 1])
                mm(0, banks[0][:, 0:n], LUd[:, jc], lin1[:, lt0:lt0 + n], True, True, ["LUd", "lin1"])
                sigmoid(SG, banks[0][:, 0:n], ["B0", "hbias"], [rn("T4"), "B0"], bias=hbias[:, KC + j:KC + j + 1])
                mm(1, banks[1][:, 0:n], WGU[:, jc], lin2[:, lt0:lt0 + n], True, True, ["WGU", "lin2"])
                act(lambda e: e.activation(G_, banks[1][:, 0:n], AF.Copy), r=["B1"], w=[rn("T5"), "B1"])
                KKN, KM, RI = T(10), T(11), T(9)
                act(lambda e: e.activation(KKN, K_, AF.Copy, scale=vec[:, V_KK + j:V_KK + j + 1]), r=[rn("T1"), "vec"], w=[rn("T10")])
                act(lambda e: e.activation(RI, KKN, AF.Square), r=[rn("T10")], w=[rn("T9")])
                mm(2, banks[2][:, 0:n], bd1[:], RI, True, True, ["bd1", rn("T9")])
                dve(lambda e: e.tensor_scalar(RI, banks[2][:, 0:n], 1e-24, None, ALU.max), r=["B2"], w=[rn("T9"), "B2"])
                act(lambda e: e.activation(RI, RI, AF.Ln), r=[rn("T9")], w=[rn("T9")])
                act(lambda e: e.activation(RI, RI, AF.Exp, scale=-0.5), r=[rn("T9")], w=[rn("T9")])
                pool(lambda e: e.tensor_tensor(KKN, KKN, RI, ALU.mult), r=[rn("T10"), rn("T9")], w=[rn("T10")])
                dve(lambda e: e.tensor_scalar(KM, A_, vec[:, V_KA + j:V_KA + j + 1], omka[:, j:j + 1], ALU.mult, ALU.add),
                    r=[rn("T3"), "vec", "omka"], w=[rn("T11")])
                pool(lambda e: e.tensor_tensor(KM, KM, K_, ALU.mult), r=[rn("T11"), rn("T1")], w=[rn("T11")])
                BON = T(1)
                pool(lambda e: e.tensor_tensor(BON, R, KM, ALU.mult), r=[rn("T0"), rn("T11")], w=[rn("T1")])
                mm(2, banks[2][:, 0:n], bdrk[:], BON, True, True, ["bdrk", rn("T1")])
                dve(lambda e: e.tensor_tensor(BON, banks[2][:, 0:n], V_, ALU.mult), r=["B2", rn("T2")], w=[rn("T1"), "B2"])
                B_ = T(3)
                pool(lambda e: e.tensor_tensor(B_, B_, KKN, ALU.mult), r=[rn("T3"), rn("T10")], w=[rn("T3")])
                if smp:
                    return
                NCH = n // 128
                KT, BT, KTP, BTP, KKT, RT, VBF, RK, BP, WC = (b[k_] for k_ in ("KT", "BT", "KTP", "BTP", "KKT", "RT", "VBF", "RK", "BP", "WC"))
                CUM, E1, E2, E3 = T(6), T(7), T(8), T(9)
                dve(lambda e: e.tensor_tensor_scan(CUM, resetm[:, 0:n], SG, 0.0, ALU.mult, ALU.add), r=["resetm", rn("T4")], w=[rn("T6")])
                act(lambda e: e.activation(E1, CUM, AF.Exp, scale=-LAM), r=[rn("T6")], w=[rn("T7")])
                act(lambda e: e.activation(E2, CUM, AF.Exp, scale=LAM), r=[rn("T6")], w=[rn("T8")])
                pool(lambda e: e.tensor_tensor(E3, CUM, SG, ALU.subtract), r=[rn("T6"), rn("T4")], w=[rn("T9")])
                act(lambda e: e.activation(E3, E3, AF.Exp, scale=-LAM), r=[rn("T9")], w=[rn("T9")])
                KF, BF = T(6), T(4)
                pool(lambda e: e.tensor_tensor(KF, KM, E2, ALU.mult), r=[rn("T11"), rn("T8")], w=[rn("T6")])
                pool(lambda e: e.tensor_tensor(BF, B_, E2, ALU.mult), r=[rn("T3"), rn("T8")], w=[rn("T4")])
                act(lambda e: e.activation(KT[:, 0:n], KF, AF.Copy), r=[rn("T6")], w=[rn("KT")])
                act(lambda e: e.activation(BT[:, 0:n], BF, AF.Copy), r=[rn("T4")], w=[rn("BT")])
                v3 = lambda a: a.rearrange("p (c t) -> p c t", t=128)
                v3E1 = v3(E1)
                dve(lambda e: e.tensor_copy(WC[:, 0:NCH], v3E1[:, :, 127]), r=[rn("T7")], w=[rn("WC")])
                WCb = v3E1[:, :, 127:128].broadcast_to([128, NCH, 128])
                dve(lambda e: e.tensor_tensor(v3(KTP[:, 0:n]), v3(KF), WCb, ALU.mult), r=[rn("T6"), rn("T7")], w=[rn("KTP")])
                dve(lambda e: e.tensor_tensor(v3(BTP[:, 0:n]), v3(BF), WCb, ALU.mult), r=[rn("T4"), rn("T7")], w=[rn("BTP")])
                pool(lambda e: e.tensor_tensor(KKT[:, 0:n], KKN, E3, ALU.mult), r=[rn("T10"), rn("T9")], w=[rn("KKT")])
                pool(lambda e: e.tensor_tensor(RT[:, 0:n], R, E1, ALU.mult), r=[rn("T0"), rn("T7")], w=[rn("RT")])
                act(lambda e: e.activation(VBF[:, 0:n], V_, AF.Copy), r=[rn("T2")], w=[rn("VBF")])
                pool(lambda e: e.tensor_copy(RK[0:64, :, 0, :], v3(KKT[0:64, 0:n])), r=[rn("KKT")], w=[rn("RK")])
                act(lambda e: e.activation(RK[64:128, :, 1, :], v3(KKT[64:128, 0:n]), AF.Copy), r=[rn("KKT")], w=[rn("RK")])
                pool(lambda e: e.tensor_copy(RK[0:64, :, 2, :], v3(RT[0:64, 0:n])), r=[rn("RT")], w=[rn("RK")])
                act(lambda e: e.activation(RK[64:128, :, 3, :], v3(RT[64:128, 0:n]), AF.Copy), r=[rn("RT")], w=[rn("RK")])
                pool(lambda e: e.tensor_copy(BP[0:64, :, 0, :], v3(BT[0:64, 0:n])), r=[rn("BT")], w=[rn("BP")])
                act(lambda e: e.activation(BP[64:128, :, 1, :], v3(BT[64:128, 0:n]), AF.Copy), r=[rn("BT")], w=[rn("BP")])

            def phase_B(j, t0, n, tiles0, sx, W, first_tile, last_unit_of_j):
                b = BUF[sx]
                Tp = b["Tp"]
                rn = lambda nm: "%s_%d" % (nm, sx)
                T = lambda i, n_=n: Tp[:, i, 0:n_]
                smp = (n == NS)
                H32 = Hst[:, j, :]
                R, V_, SG, G_, KKN, KM, B_, BON = T(0), T(2), T(4), T(5), T(10), T(11), T(3), T(1)
                Y_ = T(8)
                if first_tile:
                    act(lambda e: e.activation(HBD[:, :], H32, AF.Copy), r=["Hst"], w=["HBD"])
                if not smp:
                    wkv_chunks(j, n, sx, b, Y_, rn, H32)
                else:
                    wkv_sample(j, n, R, V_, SG, KKN, KM, B_, Y_, rn)
                if last_unit_of_j:
                    dma_out(wkvP_d[j, 0:64, :], Hst[0:64, j, 0:64], "Hst", "D_o_wkvPa%d" % j)
                    dma_out(wkvP_d[j, 64:128, :], Hst[64:128, j, 64:128], "Hst", "D_o_wkvPb%d" % j)
                S.capture.append("SPLIT")
                YC, SQ = T(7), T(9)
                mm(2, banks[2][:, 0:n], bdm[:], Y_, True, True, ["bdm", rn("T8")])
                dve(lambda e: e.tensor_tensor(YC, Y_, banks[2][:, 0:n], ALU.subtract), r=[rn("T8"), "B2"], w=[rn("T7"), "B2"])
                act(lambda e: e.activation(SQ, YC, AF.Square), r=[rn("T7")], w=[rn("T9")])
                mm(2, banks[2][:, 0:n], bdm[:], SQ, True, True, ["bdm", rn("T9")])
                dve(lambda e: e.tensor_scalar(SQ, banks[2][:, 0:n], GN_EPS, None, ALU.add), r=["B2"], w=[rn("T9"), "B2"])
                act(lambda e: e.activation(SQ, SQ, AF.Ln), r=[rn("T9")], w=[rn("T9")])
                act(lambda e: e.activation(SQ, SQ, AF.Exp, scale=-0.5), r=[rn("T9")], w=[rn("T9")])
                dve(lambda e: e.tensor_tensor(YC, YC, SQ, ALU.mult), r=[rn("T7"), rn("T9")], w=[rn("T7")])
                act(lambda e: e.activation(YC, YC, AF.Identity, scale=vec[:, V_GNW + j:V_GNW + j + 1], bias=vec[:, V_GNB + j:V_GNB + j + 1]),
                    r=[rn("T7"), "vec"], w=[rn("T7")])
                pool(lambda e: e.tensor_tensor(YC, YC, BON, ALU.add), r=[rn("T7"), rn("T1")], w=[rn("T7")])
                pool(lambda e: e.tensor_tensor(YC, YC, G_, ALU.mult), r=[rn("T7"), rn("T5")], w=[rn("T7")])
                proj(0, W["ga"][0], W["ga"][1], tiles0, t0, n)
                sigmoid(SQ, banks[0][:, 0:n], ["B0"], [rn("T9"), "B0"])
                dve(lambda e: e.tensor_tensor(YC, YC, SQ, ALU.mult), r=[rn("T7"), rn("T9")], w=[rn("T7")])
                UU, Y1 = T(6), T(10)
                CUE, MG = b["CUE"], b["MG"]
                proj(1, W["cu"][0], W["cu"][1], tiles0, t0, n)
                act(lambda e: e.activation(UU, banks[1][:, 0:n], AF.Copy), r=["B1"], w=[rn("T6"), "B1"])
                proj(0, W["cc"][0], W["cc"][1], tiles0, t0, n)
                cw = lambda i: vec[:, V_CW + 8 * i + j:V_CW + 8 * i + j + 1]
                if not smp:
                    dve(lambda e: e.tensor_copy(CUE[:, 0:2], ccar[:, j, :]), r=["ccar"], w=[rn("CUE")])
                    dve(lambda e: e.tensor_tensor(CUE[:, 2:2 + n], banks[0][:, 0:n], UU, ALU.mult), r=["B0", rn("T6")], w=[rn("CUE"), "B0"])
                    dve(lambda e: e.tensor_copy(ccar[:, j, :], CUE[:, n:n + 2]), r=[rn("CUE")], w=["ccar"])
                    act(lambda e: e.activation(Y1, CUE[:, 0:n], AF.Copy, scale=cw(0)), r=[rn("CUE"), "vec"], w=[rn("T10")])
                    dve(lambda e: e.scalar_tensor_tensor(Y1, CUE[:, 1:n + 1], cw(1), Y1, ALU.mult, ALU.add), r=[rn("CUE"), "vec", rn("T10")], w=[rn("T10")])
                    dve(lambda e: e.scalar_tensor_tensor(Y1, CUE[:, 2:n + 2], cw(2), Y1, ALU.mult, ALU.add), r=[rn("CUE"), "vec", rn("T10")], w=[rn("T10")])
                    if t0 + n == NP_:
                        pool(lambda e: e.tensor_copy(convout[:, j, :, 0], CUE[:, n:n + 2]), r=[rn("CUE")], w=["convout"])
                else:
                    dve(lambda e: e.tensor_tensor(CUE[:, 0:n], banks[0][:, 0:n], UU, ALU.mult), r=["B0", rn("T6")], w=[rn("CUE"), "B0"])
                    dve(lambda e: e.tensor_scalar(Y1, convS[:, j, 0, :], cw(0), None, ALU.mult), r=["convS", "vec"], w=[rn("T10")])
                    dve(lambda e: e.scalar_tensor_tensor(Y1, convS[:, j, 1, :], cw(1), Y1, ALU.mult, ALU.add), r=["convS", "vec", rn("T10")], w=[rn("T10")])
                    dve(lambda e: e.scalar_tensor_tensor(Y1, CUE[:, 0:n], cw(2), Y1, ALU.mult, ALU.add), r=[rn("CUE"), "vec", rn("T10")], w=[rn("T10")])
                    pool(lambda e: e.tensor_copy(convout[:, j, 0, 1:17], convS[:, j, 1, :]), r=["convS"], w=["convout"])
                    pool(lambda e: e.tensor_copy(convout[:, j, 1, 1:17], CUE[:, 0:n]), r=[rn("CUE")], w=["convout"])
                proj(1, W["cb"][0], W["cb"][1], tiles0, t0, n)
                dve(lambda e: e.tensor_tensor(Y1, banks[1][:, 0:n], Y1, ALU.mult), r=["B1", rn("T10")], w=[rn("T10"), "B1"])
                proj(0, W["gb"][0], W["gb"][1], tiles0, t0, n)
                sigmoid(UU, banks[0][:, 0:n], ["B0"], [rn("T6"), "B0"])
                pool(lambda e: e.tensor_tensor(Y1, Y1, UU, ALU.mult), r=[rn("T10"), rn("T6")], w=[rn("T10")])
                dve(lambda e: e.tensor_tensor(MG[:, 0:n], YC, Y1, ALU.add), r=[rn("T7"), rn("T10")], w=[rn("MG")])
                for oc in range(KC):
                    bk = oc % 2
                    mm(bk, banks[bk][:, 0:n], W["o"][0][:, oc * 128:(oc + 1) * 128], MG[:, 0:n], True, True, [W["o"][1], rn("MG")])
                    o_ = xT[:, oc, t0:t0 + n]
                    dve(lambda e: e.tensor_tensor(o_, o_, banks[bk][:, 0:n], ALU.add), r=["B%d" % bk], w=["x%d" % oc, "B%d" % bk])

            def wkv_chunks(j, n, sx, b, Y_, rn, H32):
                NCH = n // 128
                KT, BT, KTP, BTP, KKT, RT, VBF, RK, BP, WC = (b[k_] for k_ in ("KT", "BT", "KTP", "BTP", "KKT", "RT", "VBF", "RK", "BP", "WC"))
                pTb = banks[3][:].bitcast(BF16)
                bankM = (4, 6)
                bankT = (5, 3)
                idb2 = ident[:, :].unsqueeze(1).broadcast_to([128, 2, 128])
                h3 = lambda a: a.rearrange("p (h t) -> p h t", t=128)
                st = []
                for c in range(NCH):
                    cs = slice(c * 128, (c + 1) * 128)
                    pb = c % 2
                    tok = "TOK%d" % pb
                    for k_, (src, sres) in enumerate(((VBF, rn("VBF")), (KTP, rn("KTP")), (BTP, rn("BTP")))):
                        S.op("pe", lambda e: e.transpose(pTb[:, k_ * 128:(k_ + 1) * 128], src[:, cs], ident[:]),
                             reads=[sres, "ident"], writes=["B3"])
                    tokv = TOK[:, pb, 0:3, :, :].rearrange("p k s (h c) -> p k (s h) c", c=64)
                    dve(lambda e: e.tensor_copy(tokv[:, :, 0:4:3, :], pTb[:, 0:384].rearrange("p (k h c) -> p k h c", h=2, c=64)),
                        r=["B3"], w=[tok, "B3"])
                for c in range(NCH):
                    cs = slice(c * 128, (c + 1) * 128)
                    pb = c % 2
                    bM, bT = bankM[pb], bankT[pb]
                    rk = RK[:, c, :, :].rearrange("p s t -> p (s t)")
                    mm(bM, banks[bM][:, :], BT[:, cs], rk, True, True, [rn("BT"), rn("RK")])
                    dve(lambda e: e.tensor_tensor(S1[:, pb, :], banks[bM][:, :], mask4[:], ALU.mult), r=["B%d" % bM, "mask4"], w=["S1_%d" % pb, "B%d" % bM])
                    mm(bT, banks[bT][:, :], KT[:, cs], rk, True, True, [rn("KT"), rn("RK")])
                    act(lambda e: e.activation(S2[:, pb, :], banks[bT][:, :], AF.Copy), r=["B%d" % bT], w=["S2_%d" % pb, "B%d" % bT])
                    dve(lambda e: e.tensor_tensor(S2[:, pb, :], S2[:, pb, :], mask4[:], ALU.mult), r=["S2_%d" % pb, "mask4"], w=["S2_%d" % pb])
                    mm(7, banks[7][:, 0:256], KKT[:, cs], BP[:, c, :, :].rearrange("p s t -> p (s t)"), True, True, [rn("KKT"), rn("BP")])
                    dve(lambda e: e.tensor_tensor(h3(LMb[:, pb, :]), h3(banks[7][:, 0:256]), maskL[:, :].unsqueeze(1).broadcast_to([128, 2, 128]), ALU.mult),
                        r=["B7", "maskL"], w=["LM%d" % pb, "B7"])
                    dve(lambda e: e.tensor_tensor(h3(TTb[:, pb, 0, :]), idb2, h3(S1[:, pb, 0:256]), ALU.subtract),
                        r=["ident", "S1_%d" % pb], w=["TT%d_0" % pb])
                    st.append({"M": (LMb[:, pb, :], "LM%d" % pb), "MT": (S1[:, pb, 0:256], "S1_%d" % pb), "t": 0})
                for lev in range(6):
                    mb = lev % 2
                    for c in range(NCH):
                        pb = c % 2
                        bM = bankM[pb]
                        Mcur, MTcur = st[c]["M"], st[c]["MT"]
                        for h in range(2):
                            hs = slice(h * 128, (h + 1) * 128)
                            mm(bM, banks[bM][:, h * 128:(h + 1) * 128], MTcur[0][:, hs], Mcur[0][:, hs], True, True, [Mcur[1], MTcur[1]])
                            mm(bM, banks[bM][:, 256 + h * 128:256 + (h + 1) * 128], Mcur[0][:, hs], MTcur[0][:, hs], True, True, [Mcur[1], MTcur[1]])
                        mres = "MM%d_%d" % (pb, mb)
                        act(lambda e: e.activation(MMb[:, pb, mb, :], banks[bM][:, :], AF.Copy), r=["B%d" % bM], w=[mres, "B%d" % bM])
                        st[c]["M"] = (MMb[:, pb, mb, 0:256], mres)
                        st[c]["MT"] = (MMb[:, pb, mb, 256:512], mres)
                    for c in range(NCH):
                        pb = c % 2
                        bT = bankT[pb]
                        Mcur = st[c]["M"]
                        tcur = st[c]["t"]
                        for h in range(2):
                            hs = slice(h * 128, (h + 1) * 128)
                            mm(bT, banks[bT][:, hs], Mcur[0][:, hs], TTb[:, pb, tcur, hs], True, True, [Mcur[1], "TT%d_%d" % (pb, tcur)])
                        tn = 1 - tcur
                        last = (lev == 5)
                        dst = TTF[:, pb, :] if last else TTb[:, pb, tn, :]
                        dres = ("TTF%d" % pb) if last else ("TT%d_%d" % (pb, tn))
                        dve(lambda e: e.tensor_tensor(dst, banks[bT][:, 0:256], TTb[:, pb, tcur, :], ALU.add),
                            r=["B%d" % bT, "TT%d_%d" % (pb, tcur)], w=[dres, "B%d" % bT])
                        st[c]["t"] = tn
                for c in range(NCH):
                    cs = slice(c * 128, (c + 1) * 128)
                    pb = c % 2
                    tok = "TOK%d" % pb
                    VA, VB_ = TOK[:, pb, 0, 0, :], TOK[:, pb, 0, 1, :]
                    KA, KB = TOK[:, pb, 1, 0, :], TOK[:, pb, 1, 1, :]
                    BA, BB = TOK[:, pb, 2, 0, :], TOK[:, pb, 2, 1, :]
                    UA, UB = TOK[:, pb, 3, 0, :], TOK[:, pb, 3, 1, :]
                    s1, s2 = "S1_%d" % pb, "S2_%d" % pb
                    mm(7, banks[7][:, 0:128], KKT[:, cs], HBD[:, :], True, False, [rn("KKT"), "HBD"])
                    mm(7, banks[7][:, 0:64], S2[:, pb, 0:128], VA[:, 0:64], False, False, [s2, tok])
                    mm(7, banks[7][:, 64:128], S2[:, pb, 128:256], VB_[:, 64:128], False, True, [s2, tok])
                    act(lambda e: e.activation(RHSb[:, :], banks[7][:, 0:128], AF.Copy), r=["B7"], w=["RHSb", "B7"])
                    mm(7, banks[7][:, 128:192], TTF[:, pb, 0:128], RHSb[:, 0:64], True, True, ["TTF%d" % pb, "RHSb"])
                    mm(7, banks[7][:, 192:256], TTF[:, pb, 128:256], RHSb[:, 64:128], True, True, ["TTF%d" % pb, "RHSb"])
                    uv = TOK[:, pb, 3, :, :].rearrange("p s (h c) -> p (s h) c", c=64)
                    act(lambda e: e.activation(uv[:, 0:4:3, :], banks[7][:, 128:256].rearrange("p (h c) -> p h c", c=64), AF.Copy, scale=-1.0),
                        r=["B7"], w=[tok, "B7"])
                    mm(5, banks[5][:, 0:128], HBD[:, :], RT[:, cs], True, False, ["HBD", rn("RT")])
                    mm(5, banks[5][:, 0:128], VA, S2[:, pb, 256:384], False, False, [tok, s2])
                    mm(5, banks[5][:, 0:128], VB_, S2[:, pb, 384:512], False, False, [tok, s2])
                    mm(5, banks[5][:, 0:128], UA, S1[:, pb, 256:384], False, False, [tok, s1])
                    mm(5, banks[5][:, 0:128], UB, S1[:, pb, 384:512], False, True, [tok, s1])
                    mm(3, banks[3][:, 0:128], KA, VA, True, False, [tok])
                    mm(3, banks[3][:, 0:128], KB, VB_, False, False, [tok])
                    mm(3, banks[3][:, 0:128], BA, UA, False, False, [tok])
                    mm(3, banks[3][:, 0:128], BB, UB, False, True, [tok])
                    act(lambda e: e.activation(Y_[:, cs], banks[5][:, 0:128], AF.Copy), r=["B5"], w=[rn("T8"), "B5"])
                    dve(lambda e: e.scalar_tensor_tensor(H32, H32, WC[:, c:c + 1], banks[3][:, 0:128], ALU.mult, ALU.add),
                        r=["B3", rn("WC")], w=["Hst", "B3"])
                    act(lambda e: e.activation(HBD[:, :], H32, AF.Copy), r=["Hst"], w=["HBD"])

            def wkv_sample(j, n, R, V_, SG, KKN, KM, B_, Y_, rn):
                sj3 = SJ.rearrange("p (n v) -> p n v", v=64)
                pp3 = PP.rearrange("p (n v) -> p n v", v=64)
                bc = lambda a: a.unsqueeze(2).broadcast_to([128, NS, 64])
                idb = identH[:, :].unsqueeze(1).broadcast_to([128, NS, 64])
                DEC = SG
                dma_in(SJ, wkvS_d[j, :, :], "SJ", "D_wkvS")
                act(lambda e: e.activation(DEC, SG, AF.Exp, scale=-LAM), r=[rn("T4")], w=[rn("T4")])

                def bsum():
                    for hb, bk in ((0, 6), (1, 7)):
                        mm(bk, banks[bk][:, :], bd1[:], PP[:, hb * 512:(hb + 1) * 512], True, True, ["bd1", "PP"])

                def ps3(hb):
                    bk = 6 if hb == 0 else 7
                    return banks[bk][:, :].rearrange("p (n v) -> p n v", v=64), "B%d" % bk
                dve(lambda e: e.tensor_tensor(pp3, sj3, bc(KKN), ALU.mult), r=["SJ", rn("T10")], w=["PP"])
                bsum()
                dve(lambda e: e.tensor_tensor(sj3, sj3, bc(DEC), ALU.mult), r=[rn("T4")], w=["SJ"])
                for hb in range(2):
                    p3, pr = ps3(hb)
                    ns = slice(hb * 8, hb * 8 + 8)
                    dve(lambda e: e.tensor_tensor(pp3[:, ns, :], p3, bc(B_)[:, ns, :], ALU.mult), r=[pr, rn("T3")], w=["PP", pr])
                dve(lambda e: e.tensor_tensor(SJ, SJ, PP, ALU.subtract), r=["PP"], w=["SJ"])
                dve(lambda e: e.tensor_tensor(pp3, bc(V_), idb, ALU.mult), r=[rn("T2"), "identH"], w=["PP"])
                bsum()
                for hb in range(2):
                    p3, pr = ps3(hb)
                    ns = slice(hb * 8, hb * 8 + 8)
                    dve(lambda e: e.tensor_tensor(pp3[:, ns, :], p3, bc(KM)[:, ns, :], ALU.mult), r=[pr, rn("T11")], w=["PP", pr])
                dve(lambda e: e.tensor_tensor(SJ, SJ, PP, ALU.add), r=["PP"], w=["SJ"])
                dma_out(wkvSo_d[j, :, :], SJ, "SJ", "D_o_wkvS%d" % j)
                dve(lambda e: e.tensor_tensor(pp3, sj3, bc(R), ALU.mult), r=["SJ", rn("T0")], w=["PP"])
                bsum()
                for hb in range(2):
                    p3, pr = ps3(hb)
                    ns = slice(hb * 8, hb * 8 + 8)
                    dve(lambda e: e.tensor_tensor(pp3[:, ns, :], p3, idb[:, ns, :], ALU.mult), r=[pr, "identH"], w=["PP", pr])
                dve(lambda e: e.tensor_reduce(Y_, pp3, AX.X, ALU.add), r=["PP"], w=[rn("T8")])

            def interleave(a, b_):
                out = []
                i = k = 0
                while i < len(a) or k < len(b_):
                    if k >= len(b_) or (i < len(a) and i * len(b_) <= k * len(a)):
                        out.append(a[i]); i += 1
                    else:
                        out.append(b_[k]); k += 1
                return out

            MT = [(t * NM, NM) for t in range(NP_ // NM)] + [(NP_, NS)]
            NTILES = [TILES[0:2], TILES[2:5]]
            unit_no = 0
            for half, tiles in enumerate([MT[0:4], MT[4:9]]):
                tiles0 = tiles[0][0]
                rmsnorm(V_NMIX, NTILES[half], hTh, "hTh", sq_ap, nscr[:, 0, :], rsres="nscr0")
                for q, off in ((24, OFF_L1), (25, OFF_L2)):
                    wt, wres = wload(w_in[:, off:off + 128], "k8", slot=0)
                    for (t0, n) in NTILES[half]:
                        lt0 = t0 - tiles0
                        dst = nscr[:, 0, 0:n]
                        proj(0, wt, wres, tiles0, t0, n)
                        shift_evac(0, q, n, n == NS, dst, "nscr0", nscr[:, 1, :], nscr[:, 2, :], "nscr1", "nscr2")
                        if q == 24:
                            act(lambda e: e.activation(lin1[0:64, lt0:lt0 + n], dst[0:64, :], AF.Tanh), r=["nscr0"], w=["lin1"])
                            act(lambda e: e.activation(lin1[64:128, lt0:lt0 + n], dst[64:128, :], AF.Copy), r=["nscr0"], w=["lin1"])
                        else:
                            act(lambda e: e.activation(dst, dst, AF.Tanh, scale=0.5), r=["nscr0"], w=["nscr0"])
                            dve(lambda e: e.tensor_scalar(lin2[:, lt0:lt0 + n], dst, 0.5, 0.5, ALU.mult, ALU.add), r=["nscr0"], w=["lin2"])
                pend = []
                for j in range(KC):
                    W = {}
                    S.capture = []
                    for si, (nm, off) in enumerate((("r", OFF_R), ("k", OFF_K), ("v", OFF_V))):
                        W[nm] = wload(w_in[:, off + j * 128:off + (j + 1) * 128], "k8", slot=si)
                    wlA = S.capture
                    S.capture = []
                    for si, (nm, off) in enumerate((("ga", OFF_GA), ("cu", OFF_CU), ("cc", OFF_CC), ("cb", OFF_CB), ("gb", OFF_GB))):
                        W[nm] = wload(w_in[:, off + j * 128:off + (j + 1) * 128], "k8", slot=3 + si)
                    W["o"] = wload(w_out[j * 128:(j + 1) * 128, :], "row", slot=8)
                    wlB = S.capture
                    S.capture = None
                    for ti, (t0, n) in enumerate(tiles):
                        sx = unit_no % 2
                        unit_no += 1
                        S.capture = []
                        phase_A(j, t0, n, tiles0, sx, W, ti == 0)
                        la = (wlA if ti == 0 else []) + S.capture
                        S.capture = []
                        phase_B(j, t0, n, tiles0, sx, W, ti == 0, half == 1 and ti == len(tiles) - 1)
                        lb = S.capture
                        S.capture = None
                        sp_ = lb.index("SPLIT")
                        lb1, lb2 = lb[:sp_], (wlB if ti == 0 else []) + lb[sp_ + 1:]
                        pend.append([la, lb1, lb2])
                K_ = len(pend)
                win = []
                for k in range(K_ + 2):
                    l2 = (pend[k - 2][2] if 0 <= k - 2 < K_ else []) + (pend[k][0] if k < K_ else [])
                    l1 = pend[k - 1][1] if 0 <= k - 1 < K_ else []
                    if LIST_SCHED:
                        win += l1 + l2
                        if (k % WIN_STEPS) == WIN_STEPS - 1 or k == K_ + 1:
                            S.sched_flush(win)
                            win = []
                    elif INTERLEAVE:
                        S.merge_flush(l2, l1)
                    else:
                        S.flush(l1 + l2)
            pool(lambda e: e.tensor_copy(shout[:, :, 0], carr[:, :]), r=["carr%d" % q for q in range(26)], w=["shout"])
            dma_out(shout_d[:, :], shout[:].rearrange("p a b -> p (a b)"), "shout", "D_o_sh")
            dma_out(convout_d[:, :], convout[:].rearrange("p a b c -> p (a b c)"), "convout", "D_o_cv")

        def ple():
            S.barrier()
            hT = carve16(0, KC * NTOK).rearrange("p (k t) -> p k t", t=NTOK)
            pTb = carve16(8256, 2 * NTOK).rearrange("p (k t) -> p k t", t=NTOK)
            sq_ap = carve16(10320, 1024).rearrange("p (a t) -> p a t", t=512)
            rs_ap = carve32(10832, 512)
            gts = [carve32(11344, 512), carve32(15984, 512)]
            it_ = 0
            pst = carve32(11856, 2 * NTOK).rearrange("p (k t) -> p k t", t=NTOK)
            rmsnorm(V_NPLE, FT, hT, "hT", sq_ap, rs_ap)
            for k in range(2):
                dma_in(pst[:, k, :], pT_d[k * 128:(k + 1) * 128, :], "pst%d" % k, "D_pst%d" % k)
                pool(lambda e, k=k: e.tensor_copy(pTb[:, k, :], pst[:, k, :]), r=["pst%d" % k], w=["pTb"])
            for oc in range(KC):
                wg, wgr = wload(w_pg[:, oc * 128:(oc + 1) * 128], "k8")
                wp, wpr = wload(w_pp[:, oc * 128:(oc + 1) * 128], "k2")
                for (t0, n) in FT:
                    b0 = (it_ % 2) * 2
                    b1 = b0 + 1
                    gt = gts[it_ % 2]
                    gres = "gt%d" % (it_ % 2)
                    it_ += 1
                    for kc in range(KC):
                        mm(b0, banks[b0][:, 0:n], wg[:, kc, :], hT[:, kc, t0:t0 + n], kc == 0, kc == KC - 1, [wgr, "hT"])
                    for k in range(2):
                        mm(b1, banks[b1][:, 0:n], wp[:, k, :], pTb[:, k, t0:t0 + n], k == 0, k == 1, [wpr, "pTb"])
                    sigmoid(gt[:, 0:n], banks[b0][:, 0:n], ["B%d" % b0], [gres, "B%d" % b0])
                    dve(lambda e: e.tensor_tensor(gt[:, 0:n], gt[:, 0:n], banks[b1][:, 0:n], ALU.mult), r=[gres, "B%d" % b1], w=[gres, "B%d" % b1])
                    o_ = xT[:, oc, t0:t0 + n]
                    dve(lambda e: e.tensor_tensor(o_, o_, gt[:, 0:n], ALU.add), r=[gres], w=["x%d" % oc])

        def final_norm():
            S.barrier()
            sq_ap = carve16(0, 1024).rearrange("p (a t) -> p a t", t=512)
            rs_ap = carve32(512, 512)
            rmsnorm(V_NFIN, FT, None, None, sq_ap, rs_ap, inplace=True)
            for kc in range(KC):
                dma_out(yT_d[kc * 128:(kc + 1) * 128, :], xT[:, kc, :], "x%d" % kc, "D_o_y%d" % kc)

        ffn(w_f1g, w_f1u, w_f1d, V_NF1)
        mixer()
        ffn(w_f2g, w_f2u, w_f2d, V_NF2)
        ple()
        final_norm()
        S.emit(nc, stack)
    return nc


_NC_CACHE = {}


def _fm(v):
    v = np.asarray(v, np.float32).reshape(-1, 128)
    return np.ascontiguousarray(v.T)


def kernel(x_prompt, x_sample, p_prompt, p_sample, state_wkv, state_shift, state_conv,
           norm_ffn1, w_ffn1_gate, w_ffn1_up, w_ffn1_down,
           norm_mix, w_in, mu_shift, w_decay_up, decay_bias, w_iclr_up, iclr_bias,
           w_gate_up, k_k, k_a, r_k, gn_w, gn_b, conv_w, w_out,
           norm_ffn2, w_ffn2_gate, w_ffn2_up, w_ffn2_down,
           norm_ple, w_ple_gate, w_ple_proj, norm_final):
    f = lambda a: np.ascontiguousarray(np.asarray(a, np.float32))
    x_prompt, x_sample, p_prompt, p_sample = f(x_prompt), f(x_sample), f(p_prompt), f(p_sample)
    state_wkv, state_shift, state_conv = f(state_wkv), f(state_shift), f(state_conv)
    vecs = np.concatenate([
        _fm(norm_ffn1[0]), _fm(norm_mix[0]), _fm(norm_ffn2[0]), _fm(norm_ple[0]), _fm(norm_final),
        _fm(decay_bias[0]), _fm(iclr_bias[0]), _fm(k_k[0]), _fm(k_a[0]), _fm(np.asarray(r_k[0]).reshape(-1)),
        _fm(gn_w[0]), _fm(gn_b[0]), _fm(conv_w[0, 0]), _fm(conv_w[0, 1]), _fm(conv_w[0, 2]), _fm(mu_shift[0])], axis=1)
    assert vecs.shape == (128, NV)
    shared = {
        "vecs": f(vecs),
        "w_f1g": f(w_ffn1_gate[0]), "w_f1u": f(w_ffn1_up[0]), "w_f1d": f(w_ffn1_down[0]),
        "w_f2g": f(w_ffn2_gate[0]), "w_f2u": f(w_ffn2_up[0]), "w_f2d": f(w_ffn2_down[0]),
        "w_in": f(w_in[0]), "w_du": f(w_decay_up[0]), "w_iu": f(w_iclr_up[0]), "w_gu": f(w_gate_up[0]),
        "w_out": f(w_out[0]), "w_pg": f(w_ple_gate[0]), "w_pp": f(w_ple_proj[0]),
    }
    in_maps = []
    for i in range(NCORE):
        ss = slice(i * NS, (i + 1) * NS)
        xT = np.concatenate([x_prompt[i].T, x_sample[ss, 0, :].T], axis=1)
        pT = np.concatenate([p_prompt[0, i].T, p_sample[0, ss, 0, :].T], axis=1)
        shT = state_shift[0, ss, :].T.reshape(26, 128, NS).transpose(1, 0, 2).reshape(128, 26 * NS)
        cvT = state_conv[0, ss].transpose(2, 1, 0).reshape(KC, 128, 2, NS).transpose(1, 0, 2, 3).reshape(128, KC * 2 * NS)
        sw = state_wkv[0, ss].reshape(NS, KC, 2, 64, 64).transpose(1, 2, 4, 0, 3).reshape(KC, 128, NS * 64)
        m = dict(shared)
        m.update({"xT": f(xT), "pT": f(pT), "shiftS": f(shT), "convS": f(cvT), "wkvS": f(sw)})
        in_maps.append(m)
    if "nc" not in _NC_CACHE:
        _NC_CACHE["nc"] = build_program()
    res = run_bass_kernel_spmd(_NC_CACHE["nc"], in_maps, core_ids=list(range(NCORE)))
    R = res.results
    if DEBUG:
        DBG_RESULTS.clear()
        DBG_RESULTS.update({k: np.asarray(R[0][k]).astype(np.float32) for k in DBG_NAMES})
    y_prompt = np.stack([R[i]["yT"][:, :NP_].T for i in range(NCORE)], 0)
    y_sample = np.concatenate([R[i]["yT"][:, NP_:].T for i in range(NCORE)], 0)[:, None, :]
    wkv_p = np.stack([R[i]["wkvP"].reshape(KC, 2, 64, 64).transpose(0, 1, 3, 2).reshape(16, 64, 64) for i in range(NCORE)], 0)[None]
    sh_all = [R[i]["shout"].reshape(128, 26, 17).transpose(1, 0, 2).reshape(SHIFT_COLS, 17) for i in range(NCORE)]
    sh_p = np.stack([s[:, 0] for s in sh_all], 0)[None]
    sh_s = np.concatenate([s[:, 1:].T for s in sh_all], 0)[None]
    cv_all = [R[i]["convout"].reshape(128, KC, 2, 17).transpose(1, 0, 2, 3).reshape(D, 2, 17) for i in range(NCORE)]
    cv_p = np.stack([c[:, :, 0].T for c in cv_all], 0)[None]
    cv_s = np.concatenate([c[:, :, 1:].transpose(2, 1, 0) for c in cv_all], 0)[None]
    wkv_s = np.concatenate([R[i]["wkvSo"].reshape(KC, 2, 64, NS, 64).transpose(3, 0, 1, 4, 2).reshape(NS, 16, 64, 64)
                            for i in range(NCORE)], 0)[None]
    c = lambda a: np.ascontiguousarray(a, dtype=np.float32)
    return (c(y_prompt), c(y_sample), c(wkv_p), c(sh_p), c(cv_p), c(wkv_s), c(sh_s), c(cv_s))
```
